# Optimizing a Trainium2 kernel written in Bass

```python
import math
import jax, jax.numpy as jnp
from jax import lax
import numpy as np

D_MODEL = 2048
BATCH = 2
SEQ = 4096
DEPTH = 1

BRANCH_W = D_MODEL // 2
HEAD_DIM = 128
N_HEADS_A = BRANCH_W // HEAD_DIM
DH_A = HEAD_DIM // 2
DV_A = HEAD_DIM
N_HEADS_B = BRANCH_W // HEAD_DIM
N_KV_B = N_HEADS_B // 4
GROUP_B = N_HEADS_B // N_KV_B
DH_B = HEAD_DIM
ROPE_THETA = 10000.0
GRID_W = 64
Q_BLOCK = 128
EPS = 1e-6

COL_SIZES = (
    N_HEADS_A * 2 * DH_A,
    N_HEADS_A * 2 * DH_A,
    N_HEADS_A * DV_A,
    BRANCH_W,
    N_HEADS_B * DH_B,
    N_KV_B * DH_B,
    N_KV_B * DH_B,
    BRANCH_W,
    D_MODEL,
    D_MODEL,
)
COL_TOTAL = int(sum(COL_SIZES))
SPLIT_POINTS = tuple(int(c) for c in np.cumsum(COL_SIZES)[:-1])

kernel_name = "hybrid_gated_diffattn_axialgqa_block"


def rmsnorm(x, g):
    xf = x.astype(jnp.float32)
    y = xf * lax.rsqrt(jnp.mean(xf * xf, axis=-1, keepdims=True) + EPS)
    return (y * g.astype(jnp.float32)).astype(x.dtype)


def rope_angles(pos, dim):
    inv = 1.0 / (ROPE_THETA ** (jnp.arange(0, dim, 2, dtype=jnp.float32) / dim))
    return pos.astype(jnp.float32)[:, None] * inv[None, :]


def apply_rope(x, ang):
    cos = jnp.cos(ang)[None, :, None, :]
    sin = jnp.sin(ang)[None, :, None, :]
    xf = x.astype(jnp.float32)
    x1, x2 = jnp.split(xf, 2, axis=-1)
    return jnp.concatenate([x1 * cos - x2 * sin, x2 * cos + x1 * sin], axis=-1).astype(x.dtype)


def to_blocks(t):
    s, d = t.shape[-2], t.shape[-1]
    t = t.reshape(t.shape[:-2] + (s // Q_BLOCK, Q_BLOCK, d))
    return jnp.moveaxis(t, -3, 0)


def from_blocks(t):
    t = jnp.moveaxis(t, 0, -3)
    return t.reshape(t.shape[:-3] + (t.shape[-3] * Q_BLOCK, t.shape[-1]))


def diff_attention(q1, q2, k1, k2, v, lam):
    scale = DH_A ** -0.5

    def block(args):
        q1b, q2b = args
        s1 = jnp.einsum('bhqd,bhkd->bhqk', q1b, k1, preferred_element_type=jnp.float32) * scale
        s2 = jnp.einsum('bhqd,bhkd->bhqk', q2b, k2, preferred_element_type=jnp.float32) * scale
        p = jax.nn.softmax(s1, axis=-1) - lam * jax.nn.softmax(s2, axis=-1)
        return jnp.einsum('bhqk,bhkd->bhqd', p.astype(v.dtype), v)

    o = lax.map(block, (to_blocks(q1), to_blocks(q2)))
    return from_blocks(o)


def gqa_attention(q, k, v):
    scale = DH_B ** -0.5

    def block(qb):
        s = jnp.einsum('bkgqd,bksd->bkgqs', qb, k, preferred_element_type=jnp.float32) * scale
        p = jax.nn.softmax(s, axis=-1)
        return jnp.einsum('bkgqs,bksd->bkgqd', p.astype(v.dtype), v)

    o = lax.map(block, to_blocks(q))
    return from_blocks(o)


def setup_inputs(seed: int = 0) -> dict:
    key = jax.random.key(seed)
    ks = jax.random.split(key, 16)
    f32 = jnp.float32
    x = jax.random.normal(ks[0], (BATCH, SEQ, D_MODEL), f32)
    norm_g = 1.0 + 0.01 * jax.random.normal(ks[1], (DEPTH, D_MODEL), f32)
    w_in = jax.random.normal(ks[2], (DEPTH, D_MODEL, COL_TOTAL), f32) * D_MODEL ** -0.5
    lambda_q1 = 0.1 * jax.random.normal(ks[3], (DEPTH, DH_A), f32)
    lambda_k1 = 0.1 * jax.random.normal(ks[4], (DEPTH, DH_A), f32)
    lambda_q2 = 0.1 * jax.random.normal(ks[5], (DEPTH, DH_A), f32)
    lambda_k2 = 0.1 * jax.random.normal(ks[6], (DEPTH, DH_A), f32)
    subln_g = 1.0 + 0.01 * jax.random.normal(ks[7], (DEPTH, DV_A), f32)
    q_norm_g = 1.0 + 0.01 * jax.random.normal(ks[8], (DEPTH, DH_B), f32)
    k_norm_g = 1.0 + 0.01 * jax.random.normal(ks[9], (DEPTH, DH_B), f32)
    w_out_a = jax.random.normal(ks[10], (DEPTH, BRANCH_W, D_MODEL), f32) * BRANCH_W ** -0.5
    w_out_b = jax.random.normal(ks[11], (DEPTH, BRANCH_W, D_MODEL), f32) * BRANCH_W ** -0.5
    w_o = jax.random.normal(ks[12], (DEPTH, D_MODEL, D_MODEL), f32) * D_MODEL ** -0.5
    final_g = 1.0 + 0.01 * jax.random.normal(ks[13], (D_MODEL,), f32)
    return {"x": x, "norm_g": norm_g, "w_in": w_in,
            "lambda_q1": lambda_q1, "lambda_k1": lambda_k1,
            "lambda_q2": lambda_q2, "lambda_k2": lambda_k2,
            "subln_g": subln_g, "q_norm_g": q_norm_g, "k_norm_g": k_norm_g,
            "w_out_a": w_out_a, "w_out_b": w_out_b, "w_o": w_o, "final_g": final_g}


def reference(x, norm_g, w_in, lambda_q1, lambda_k1, lambda_q2, lambda_k2,
              subln_g, q_norm_g, k_norm_g, w_out_a, w_out_b, w_o, final_g):
    B, S, _ = x.shape
    rows = S // GRID_W
    ang_1d = rope_angles(jnp.arange(S), DH_A)
    pos_row = jnp.repeat(jnp.arange(rows), GRID_W)
    pos_col = jnp.tile(jnp.arange(GRID_W), rows)
    ang_row = rope_angles(pos_row, DH_B // 2)
    ang_col = rope_angles(pos_col, DH_B // 2)

    def axial_rope(t):
        half = DH_B // 2
        return jnp.concatenate([apply_rope(t[..., :half], ang_row),
                                apply_rope(t[..., half:], ang_col)], axis=-1)

    for l in range(DEPTH):
        lam_init = 0.8 - 0.6 * math.exp(-0.3 * l)
        h = rmsnorm(x, norm_g[l])
        proj = jnp.einsum('bsd,dc->bsc', h, w_in[l])
        qa, ka, va, za, qb, kb, vb, zb, ga, gb = jnp.split(proj, SPLIT_POINTS, axis=-1)

        qa = apply_rope(qa.reshape(B, S, 2 * N_HEADS_A, DH_A), ang_1d)
        ka = apply_rope(ka.reshape(B, S, 2 * N_HEADS_A, DH_A), ang_1d)
        qa = qa.reshape(B, S, N_HEADS_A, 2, DH_A).transpose(3, 0, 2, 1, 4)
        ka = ka.reshape(B, S, N_HEADS_A, 2, DH_A).transpose(3, 0, 2, 1, 4)
        va = va.reshape(B, S, N_HEADS_A, DV_A).transpose(0, 2, 1, 3)
        lam = (jnp.exp(jnp.sum(lambda_q1[l].astype(jnp.float32) * lambda_k1[l].astype(jnp.float32)))
               - jnp.exp(jnp.sum(lambda_q2[l].astype(jnp.float32) * lambda_k2[l].astype(jnp.float32)))
               + lam_init)
        oa = diff_attention(qa[0], qa[1], ka[0], ka[1], va, lam)
        oa = rmsnorm(oa, subln_g[l]) * (1.0 - lam_init)
        oa = oa.transpose(0, 2, 1, 3).reshape(B, S, BRANCH_W)
        ya = jnp.einsum('bsc,cd->bsd', oa * jax.nn.silu(za), w_out_a[l])

        qb = axial_rope(rmsnorm(qb.reshape(B, S, N_HEADS_B, DH_B), q_norm_g[l]))
        kb = axial_rope(rmsnorm(kb.reshape(B, S, N_KV_B, DH_B), k_norm_g[l]))
        vb = vb.reshape(B, S, N_KV_B, DH_B)
        qb = qb.reshape(B, S, N_KV_B, GROUP_B, DH_B).transpose(0, 2, 3, 1, 4)
        kb = kb.transpose(0, 2, 1, 3)
        vb = vb.transpose(0, 2, 1, 3)
        ob = gqa_attention(qb, kb, vb)
        ob = ob.transpose(0, 3, 1, 2, 4).reshape(B, S, BRANCH_W)
        yb = jnp.einsum('bsc,cd->bsd', ob * jax.nn.silu(zb), w_out_b[l])

        merged = jax.nn.sigmoid(ga) * ya + jax.nn.sigmoid(gb) * yb
        x = x + jnp.einsum('bsd,de->bse', merged, w_o[l])

    return rmsnorm(x, final_g)
```

```python
import math
from collections import deque
from contextlib import ExitStack

import numpy as np
import concourse.bass as bass
import concourse.mybir as mybir
from concourse.bass_utils import run_bass_kernel_spmd

F32 = mybir.dt.float32
BF16 = mybir.dt.bfloat16
AF = mybir.ActivationFunctionType
ALU = mybir.AluOpType

D = 2048
KC = 16
S = 4096
NOWN = 1024
CH = 512
NCH = S // CH
COLS = 10752
EPS = 1e-6
LAM_INIT = 0.8 - 0.6 * math.exp(-0.3 * 0)
C_QA, C_KA, C_VA, C_ZA, C_QB, C_KB, C_VB, C_ZB, C_GA, C_GB = 0, 1024, 2048, 3072, 4096, 5120, 5376, 5632, 6656, 8704
SC_A = 64 ** -0.5
SC_B = 128 ** -0.5


class Tracker:
    def __init__(self, nc, stack):
        self.nc = nc
        self.stack = stack
        self.eng = {"pe": nc.tensor, "act": nc.scalar, "dve": nc.vector, "pool": nc.gpsimd, "sp": nc.sync}
        self.sems = {}
        self.cnt = {}
        self.waited = {e: {} for e in self.eng}
        self.lastw = {}
        self.readers = {}
        for e in self.eng:
            self._sem("e_" + e)

    def _sem(self, name):
        if name not in self.sems:
            self.sems[name] = self.stack.enter_context(self.nc.semaphore(name))
            self.cnt[name] = 0
        return self.sems[name]

    def _wait_for(self, e, toks):
        need = {}
        for t in toks:
            if t is None:
                continue
            s, v = t
            if v > need.get(s, 0):
                need[s] = v
        for s, v in need.items():
            if self.waited[e].get(s, 0) >= v:
                continue
            if e == "pe" and s == "e_pe":
                continue
            self.eng[e].wait_ge(self.sems[s], v)
            self.waited[e][s] = v

    def _deps(self, reads, writes, e=None):
        toks = []
        for k in reads:
            toks.append(self.lastw.get(k))
        own = "e_" + e if e else None
        for k in writes:
            for t in [self.lastw.get(k)] + list(self.readers.get(k, ())):
                if t is not None and t[0] != own:
                    toks.append(t)
        return toks

    def _record(self, tok, reads, writes):
        for k in reads:
            self.readers.setdefault(k, []).append(tok)
        for k in writes:
            self.lastw[k] = tok
            self.readers[k] = []

    @staticmethod
    def _excl(reads, writes):
        ps_reads = [k for k in reads if isinstance(k, tuple) and k and k[0] == "ps"]
        if not ps_reads:
            return reads, writes
        return [k for k in reads if k not in ps_reads], list(writes) + ps_reads

    def op(self, e, fn, reads=(), writes=()):
        reads, writes = self._excl(reads, writes)
        self._wait_for(e, self._deps(reads, writes, e))
        ins = fn(self.eng[e])
        s = "e_" + e
        self.cnt[s] += 1
        ins.then_inc(self.sems[s], 1)
        self._record((s, self.cnt[s]), reads, writes)

    def dma(self, q, out, in_, reads, writes, sem):
        self._sem(sem)
        self._wait_for(q, self._deps(reads, writes))
        ins = self.eng[q].dma_start(out=out, in_=in_)
        self.cnt[sem] += 16
        ins.then_inc(self.sems[sem], 16)
        self._record((sem, self.cnt[sem]), reads, writes)

    def barrier(self):
        toks = [(s, c) for s, c in self.cnt.items() if c > 0]
        for e in self.eng:
            self._wait_for(e, toks)
        self.lastw.clear()
        self.readers.clear()


def build_nc(debug=None):
    nc = bass.Bass("TRN2", target_bir_lowering=False)
    dt = nc.dram_tensor
    xkv = dt("xkv", [S, D], F32, kind="ExternalInput").ap()
    w_in = dt("w_in", [D, COLS], F32, kind="ExternalInput").ap()
    w_oa = dt("w_out_a", [1024, D], F32, kind="ExternalInput").ap()
    w_ob = dt("w_out_b", [1024, D], F32, kind="ExternalInput").ap()
    w_o = dt("w_o", [D, D], F32, kind="ExternalInput").ap()
    norm_g = dt("norm_g", [D], F32, kind="ExternalInput").ap()
    final_g = dt("final_g", [D], F32, kind="ExternalInput").ap()
    lam4 = dt("lam4", [256], F32, kind="ExternalInput").ap()
    gvecs = dt("gvecs", [128, 3], F32, kind="ExternalInput").ap()
    tabs = {n: dt(n, [128, S], F32, kind="ExternalInput").ap() for n in ("cosA", "sinA", "cosB", "sinB")}
    cmats = dt("cmats", [128, 3, 128], F32, kind="ExternalInput").ap()
    out = dt("out", [NOWN, D], F32, kind="ExternalOutput").ap()
    kT_scr = dt("kT_scr", [10, 128, S], BF16, kind="Internal").ap()
    v_scr = dt("v_scr", [10, 128, 32, 128], BF16, kind="Internal").ap()
    dbg = {}
    if debug:
        dbg["qT"] = dt("dbg_qT", [128, 16, NOWN], F32, kind="ExternalOutput").ap()
        dbg["zs"] = dt("dbg_zs", [128, 16, NOWN], F32, kind="ExternalOutput").ap()
        dbg["kT"] = dt("dbg_kT", [10, 128, S], BF16, kind="ExternalOutput").ap()
        dbg["v"] = dt("dbg_v", [10, 128, 32, 128], BF16, kind="ExternalOutput").ap()
        dbg["og"] = dt("dbg_og", [128, 16, NOWN], F32, kind="ExternalOutput").ap()
        dbg["mg"] = dt("dbg_mg", [128, 16, NOWN], F32, kind="ExternalOutput").ap()

    with ExitStack() as st:
        E = st.enter_context
        T = Tracker(nc, st)
        sb = lambda name, shape, dtp: E(nc.sbuf_tensor(name, shape, dtp))

        big = sb("big", [128, 40960], BF16)
        hT_own = sb("hT_own", [128, KC, NOWN], BF16)
        bufB = sb("bufB", [128, 16384], BF16)
        arena = sb("arena", [128, 29184], BF16)
        cm = sb("cm", [128, 3, 128], BF16)
        gv = sb("gv", [128, 3], F32)
        g16 = sb("g16", [128, KC], F32)
        lamb = sb("lamb", [128, 256], F32)
        lamt = sb("lamt", [128, 8], F32)
        epsb = sb("epsb", [128, 1], F32)
        st8 = sb("st8", [128, 16], F32)

        def view(buf, off_kib, shape, dtp):
            esz = 4 if dtp == F32 else 2
            n = 1
            for s_ in shape[1:]:
                n *= s_
            a_ = int(off_kib * 1024) // 2
            ap = buf[:, a_:a_ + n * esz // 2]
            if dtp == F32:
                ap = ap.bitcast(F32)
            if len(shape) == 3:
                ap = ap.rearrange("p (a b) -> p a b", a=shape[1])
            elif len(shape) == 4:
                ap = ap.rearrange("p (a b c) -> p a b c", a=shape[1], b=shape[2])
            elif len(shape) == 5:
                ap = ap.rearrange("p (a b c d) -> p a b c d", a=shape[1], b=shape[2], c=shape[3])
            return ap

        xs = view(arena, 0, [128, 2, D], F32)
        hb2 = view(arena, 16, [128, 2, D], BF16)
        tab = view(arena, 24, [128, 4, CH], F32)
        kb = view(arena, 32, [128, 2, CH], BF16)
        t1 = view(arena, 34, [128, 2, CH], F32)
        t2 = view(arena, 38, [128, 2, CH], F32)
        lnb = view(arena, 42, [128, 2, CH], F32)
        rsb = view(arena, 46, [128, 2, CH], F32)
        sqb = view(arena, 50, [128, 2, CH], BF16)
        kout = view(arena, 52, [128, 2, CH], BF16)
        vout = view(arena, 54, [128, 1280], BF16)
        PS = [E(nc.psum_tensor("ps%d" % i, [128, 1024], F32)) for i in range(4)]

        ident = cm[:, 0, :]
        perm = cm[:, 1, :]
        ones = cm[:, 2, :]

        def psk(i, h):
            return ("ps", i, h)

        def psa(i, h):
            return PS[i][:, h * 512:(h + 1) * 512]

        T.dma("pool", cm[:], cmats, [], ["cm"], "d_cm")
        T.dma("sp", gv[:], gvecs, [], ["gv"], "d_gv")
        T.dma("sp", lamb[:], lam4.partition_broadcast(128), [], ["lamb"], "d_lam")
        with nc.allow_non_contiguous_dma(reason="tiny gain vector transpose load"):
            T.dma("sp", g16[:], norm_g.rearrange("(k p) -> p k", p=128), [], ["g16"], "d_g16")
        T.op("dve", lambda e: e.memset(epsb[:], EPS), [], ["epsb"])
        T.op("dve", lambda e: e.tensor_tensor(out=t1[:, 0, 0:64], in0=lamb[:, 0:64], in1=lamb[:, 64:128], op=ALU.mult), ["lamb"], ["t1c"])
        T.op("dve", lambda e: e.tensor_tensor(out=t1[:, 0, 64:128], in0=lamb[:, 128:192], in1=lamb[:, 192:256], op=ALU.mult), ["lamb", "t1c"], ["t1c"])
        T.op("dve", lambda e: e.reduce_sum(out=lamt[:, 0:1], in_=t1[:, 0, 0:64], axis=mybir.AxisListType.X), ["t1c"], ["lamt"])
        T.op("dve", lambda e: e.reduce_sum(out=lamt[:, 1:2], in_=t1[:, 0, 64:128], axis=mybir.AxisListType.X), ["t1c", "lamt"], ["lamt"])
        T.op("act", lambda e: e.activation(out=lamt[:, 2:4], in_=lamt[:, 0:2], func=AF.Exp), ["lamt"], ["lamt"])
        T.op("dve", lambda e: e.scalar_tensor_tensor(out=lamt[:, 4:5], in0=lamt[:, 3:4], scalar=-LAM_INIT, in1=lamt[:, 2:3], op0=ALU.add, op1=ALU.subtract), ["lamt"], ["lamt"])
        T.op("dve", lambda e: e.tensor_scalar(out=lamt[:, 5:6], in0=gv[:, 0:1], scalar1=(1.0 - LAM_INIT), scalar2=None, op0=ALU.mult), ["gv", "lamt"], ["lamt"])
        neglam = lamt[:, 4:5]
        gsub = lamt[:, 5:6]
        T.barrier()
        if debug == 0.1:
            return nc

        def load_wblock(dst3, src_w, col0, ncols, key, sem):
            T.dma("pool", dst3, src_w[:, col0:col0 + ncols].rearrange("(kc p) n -> p kc n", p=128), [], [key], sem)

        def rope_finish(src_ap, src_keys, j, cos_ap, sin_ap, tab_keys, out_ap, out_key, rot_ps, inplace=False):
            ri, rh = rot_ps
            T.op("act", lambda e: e.activation(out=kb[:, j, :], in_=src_ap, func=AF.Copy), src_keys, [("kb", j)])
            T.op("pe", lambda e: e.matmul(psa(ri, rh), lhsT=perm, rhs=kb[:, j, :], start=True, stop=True), [("kb", j), "cm"], [psk(ri, rh)])
            T.op("dve", lambda e: e.tensor_tensor(out=t1[:, j, :], in0=src_ap, in1=cos_ap, op=ALU.mult), list(src_keys) + list(tab_keys), [("t1", j)])
            T.op("dve", lambda e: e.tensor_tensor(out=t2[:, j, :], in0=psa(ri, rh), in1=sin_ap, op=ALU.mult), [psk(ri, rh)] + list(tab_keys), [("t2", j)])
            T.op("pool", lambda e: e.tensor_tensor(out=out_ap, in0=t1[:, j, :], in1=t2[:, j, :], op=ALU.add), [("t1", j), ("t2", j)], [out_key])

        def headnorm(src_ps, src_key, j, gcol, aux_ps):
            ai, ah = aux_ps
            T.op("act", lambda e: e.activation(out=sqb[:, j, :], in_=src_ps, func=AF.Square), [src_key], [("sqb", j)])
            T.op("pe", lambda e: e.matmul(psa(ai, ah), lhsT=ones, rhs=sqb[:, j, :], start=True, stop=True), [("sqb", j), "cm"], [psk(ai, ah)])
            T.op("act", lambda e: e.activation(out=lnb[:, j, 0:CH], in_=psa(ai, ah), func=AF.Ln, scale=1.0 / 128, bias=epsb[:]), [psk(ai, ah), "epsb"], [("lnb", j)])
            T.op("act", lambda e: e.activation(out=rsb[:, j, 0:CH], in_=lnb[:, j, 0:CH], func=AF.Exp, scale=-0.5), [("lnb", j)], [("rsb", j)])
            T.op("dve", lambda e: e.scalar_tensor_tensor(out=t1[:, j, :], in0=src_ps, scalar=gv[:, gcol:gcol + 1], in1=rsb[:, j, 0:CH], op0=ALU.mult, op1=ALU.mult), [src_key, ("rsb", j), "gv"], [("t1", j)])

        def proj_fm(ps_idx, wfun, rfun, reads):
            pi, ph = ps_idx

            def f(e):
                last = None
                for kc in range(KC):
                    last = e.matmul(psa(pi, ph), lhsT=wfun(kc), rhs=rfun(kc), start=(kc == 0), stop=(kc == KC - 1))
                return last
            T.op("pe", f, reads, [psk(pi, ph)])

        def rms_stats(src_ap, src_key, slot, junk_ap, junk_key):
            sc = st8[:, 4 * slot:4 * slot + 4]
            sk = ("st8", slot)
            T.op("act", lambda e: e.activation(out=junk_ap, in_=src_ap, func=AF.Square, accum_out=sc[:, 0:1]), [src_key], [junk_key, sk])
            T.op("act", lambda e: e.activation(out=sc[:, 1:2], in_=sc[:, 0:1], func=AF.Ln, scale=1.0 / D, bias=epsb[:]), [sk, "epsb"], [sk])
            T.op("act", lambda e: e.activation(out=sc[:, 2:3], in_=sc[:, 1:2], func=AF.Exp, scale=-0.5), [sk], [sk])
            return sc[:, 2:3], sk

        dq = []
        tickc = [0]

        def defer(n, fn):
            dq.append([tickc[0] + n, fn])

        def tick():
            tickc[0] += 1
            for d_ in [d_ for d_ in dq if d_[0] <= tickc[0]]:
                dq.remove(d_)
                d_[1]()

        def flush():
            while dq:
                dq.pop(0)[1]()

        wkv = big[:, 0:KC * 2560].rearrange("p (k n) -> p k n", k=KC)
        wkv_blocks = [(0, C_KA, 512), (512, C_KA + 512, 512), (1024, C_KB, 256),
                      (1280, C_VA, 512), (1792, C_VA + 512, 512), (2304, C_VB, 256)]
        for bi, (dcol, scol, n) in enumerate(wkv_blocks):
            load_wblock(wkv[:, :, dcol:dcol + n], w_in, scol, n, ("wkv", bi), "d_wkv%d" % bi)
        wkv_keys = [("wkv", bi) for bi in range(6)]
        if debug == 0.2:
            T.barrier()
            return nc
        hT_rot = bufB[:].rearrange("p (b k t) -> p b k t", b=2, k=KC)
        kcount = 0

        def tile_dst(ch, t4):
            if ch < 2:
                return hT_own, ("hT_own", ch), ch * CH + t4 * 128
            return hT_rot[:, ch % 2], ("hT_rot", ch % 2), t4 * 128

        def prepA(ch, t4):
            tt = ch * 4 + t4
            xbuf = tt % 2
            xk = ("xs", xbuf)
            T.dma("sp", xs[:, xbuf, :], xkv[tt * 128:(tt + 1) * 128, :], [], [xk], "d_xs%d" % xbuf)
            rstd, sk = rms_stats(xs[:, xbuf, :], xk, xbuf, hb2[:, xbuf, :], ("hb", xbuf))
            T.op("act", lambda e: e.activation(out=hb2[:, xbuf, :], in_=xs[:, xbuf, :], func=AF.Copy, scale=rstd), [xk, sk], [("hb", xbuf)])

        def prepB(ch, t4):
            tt = ch * 4 + t4
            xbuf = tt % 2
            dstT, dkey, tok_off = tile_dst(ch, t4)
            pst = PS[3][:].bitcast(BF16)

            def tr(e):
                last = None
                for kc in range(KC):
                    last = e.transpose(pst[:, kc * 128:(kc + 1) * 128], hb2[:, xbuf, kc * 128:(kc + 1) * 128], ident)
                return last
            T.op("pe", tr, [("hb", xbuf), "cm"], [psk(3, 0), psk(3, 1)])
            for kc in range(KC):
                T.op("dve", lambda e, kc=kc: e.tensor_scalar(out=dstT[:, kc, tok_off:tok_off + 128], in0=pst[:, kc * 128:(kc + 1) * 128],
                                                             scalar1=g16[:, kc:kc + 1], scalar2=None, op0=ALU.mult),
                     [psk(3, 0), psk(3, 1), "g16"], [dkey])

        def load_tabs(ch, which):
            for ti in which:
                tn = ("cosA", "sinA", "cosB", "sinB")[ti]
                T.dma("sp", tab[:, ti, :], tabs[tn][:, ch * CH:(ch + 1) * CH], [], [("tab", ti)], "d_tab%d" % ti)

        load_tabs(0, (0, 1, 2, 3))
        if debug == 0.31:
            prepA(0, 0)
            T.barrier()
            return nc
        for t4 in range(4):
            prepA(0, t4)
            prepB(0, t4)
        if debug == 0.3:
            T.barrier()
            return nc
        for ch in range(NCH):
            if ch < 2:
                hT = hT_own[:, :, ch * CH:(ch + 1) * CH]
                hkey = ("hT_own", ch)
            else:
                hT = hT_rot[:, ch % 2, :, :]
                hkey = ("hT_rot", ch % 2)
            for cc in range(10):
                pb = (cc % 2, 0)
                proj_fm(pb, lambda kc, cc=cc: wkv[:, kc, cc * 128:(cc + 1) * 128], lambda kc: hT[:, kc, :], [hkey] + wkv_keys)
                tick()
                if ch + 1 < NCH and cc % 2 == 0:
                    if cc // 2 < 4:
                        prepA(ch + 1, cc // 2)
                    if 1 <= cc // 2 <= 4:
                        prepB(ch + 1, cc // 2 - 1)
                j = cc % 2
                ko = kcount % 2
                kcount += 1

                def store(cc=cc, ko=ko, ch=ch):
                    T.dma("sp", kT_scr[cc, :, ch * CH:(ch + 1) * CH], kout[:, ko, :], [("kout", ko)], [], "d_kout%d" % ko)

                if cc < 8:
                    def postA(cc=cc, pb=pb, j=j, ko=ko, ch=ch, store=store):
                        rope_finish(psa(*pb), [psk(*pb)], j, tab[:, 0, :], tab[:, 1, :], [("tab", 0), ("tab", 1)], kout[:, ko, :], ("kout", ko), (cc % 2, 1))
                        store()
                        if cc == 7 and ch + 1 < NCH:
                            load_tabs(ch + 1, (0, 1))
                    defer(1, postA)
                else:
                    def postB1(cc=cc, pb=pb, j=j):
                        headnorm(psa(*pb), psk(*pb), j, 2, (cc % 2, 1))

                    def postB2(cc=cc, pb=pb, j=j, ko=ko, ch=ch, store=store):
                        rope_finish(t1[:, j, :], [("t1", j)], j, tab[:, 2, :], tab[:, 3, :], [("tab", 2), ("tab", 3)], kout[:, ko, :], ("kout", ko), (cc % 2, 1))
                        store()
                        if cc == 9 and ch + 1 < NCH:
                            load_tabs(ch + 1, (2, 3))
                    defer(1, postB1)
                    defer(2, postB2)
            for t4 in range(4):
                tt = ch * 4 + t4
                vkey = "vout"
                for blk, (c0, n) in enumerate(((1280, 512), (1792, 512), (2304, 256))):
                    pb = (2, blk % 2)

                    def f(e, c0=c0, n=n, pb=pb, t4=t4):
                        last = None
                        for kc in range(KC):
                            last = e.matmul(psa(*pb)[:, 0:n], lhsT=hT[:, kc, t4 * 128:(t4 + 1) * 128], rhs=wkv[:, kc, c0:c0 + n], start=(kc == 0), stop=(kc == KC - 1))
                        return last
                    T.op("pe", f, [hkey] + wkv_keys, [psk(*pb)])
                    tick()
                    T.op("act", lambda e, c0=c0, n=n, pb=pb: e.activation(out=vout[:, c0 - 1280:c0 - 1280 + n], in_=psa(*pb)[:, 0:n], func=AF.Copy), [psk(*pb)], [vkey])
                T.dma("sp", v_scr[:, :, tt, :].rearrange("h p d -> p h d"), vout[:].rearrange("p (h d) -> p h d", h=10), [vkey], [], "d_vout")
        flush()
        T.barrier()

        if debug == 1:
            T.dma("sp", dbg["kT"], kT_scr, [], [], "d_dbg")
            T.dma("sp", dbg["v"], v_scr, [], [], "d_dbg")
            T.barrier()
            return nc

        qT = big[:, 0:16384].rearrange("p (k t) -> p k t", k=16)
        zsT = big[:, 16384:32768].rearrange("p (k t) -> p k t", k=16)
        tab2 = view(big, 64, [128, 4, NOWN], F32)
        wst3 = [bufB[:, s * 8192:(s + 1) * 8192].rearrange("p (k n) -> p k n", k=KC) for s in range(2)]
        q_blocks = [("qA", C_QA), ("qA", C_QA + 512), ("zA", C_ZA), ("zA", C_ZA + 512),
                    ("qB", C_QB), ("qB", C_QB + 512), ("zB", C_ZB), ("zB", C_ZB + 512)]
        if debug in (1.5, 1.6, 1.7):
            q_blocks = {1.5: q_blocks[0:1], 1.6: q_blocks[2:3], 1.7: q_blocks[4:5]}[debug]
        load_wblock(wst3[0], w_in, q_blocks[0][1], 512, ("wst", 0), "d_wst0")
        for ti, tn in enumerate(("cosA", "sinA", "cosB", "sinB")):
            T.dma("sp", tab2[:, ti, :], tabs[tn][:, 0:NOWN], [], [("tab2", ti)], "d_tab%d" % ti)
        cnt = 0
        for bi, (kind, col0) in enumerate(q_blocks):
            slot = bi % 2
            if bi + 1 < len(q_blocks):
                load_wblock(wst3[(bi + 1) % 2], w_in, q_blocks[bi + 1][1], 512, ("wst", (bi + 1) % 2), "d_wst%d" % ((bi + 1) % 2))
            for c4 in range(4):
                hu_local = (bi % 2) * 4 + c4
                for tc in range(2):
                    pb = (cnt % 2, 0)
                    aux = (cnt % 2, 1)
                    j = cnt % 2
                    cnt += 1
                    proj_fm(pb, lambda kc, c4=c4, slot=slot: wst3[slot][:, kc, c4 * 128:(c4 + 1) * 128],
                            lambda kc, tc=tc: hT_own[:, kc, tc * CH:(tc + 1) * CH], [("hT_own", 0), ("hT_own", 1), ("wst", slot)])
                    tsl = slice(tc * CH, (tc + 1) * CH)
                    tick()
                    if kind == "qA":
                        def postA(pb=pb, j=j, tsl=tsl, hu_local=hu_local, tc=tc, aux=aux):
                            rope_finish(psa(*pb), [psk(*pb)], j, tab2[:, 0, tsl], tab2[:, 1, tsl], [("tab2", 0), ("tab2", 1)], qT[:, hu_local, tsl], ("qT", hu_local, tc), aux)
                        defer(1, postA)
                    elif kind == "qB":
                        def postB1(pb=pb, j=j, aux=aux):
                            headnorm(psa(*pb), psk(*pb), j, 1, aux)

                        def postB2(pb=pb, j=j, tsl=tsl, hu_local=hu_local, tc=tc, aux=aux):
                            rope_finish(t1[:, j, :], [("t1", j)], j, tab2[:, 2, tsl], tab2[:, 3, tsl], [("tab2", 2), ("tab2", 3)], qT[:, 8 + hu_local, tsl], ("qT", 8 + hu_local, tc), aux)
                        defer(1, postB1)
                        defer(2, postB2)
                    else:
                        hu = hu_local + (0 if kind == "zA" else 8)
                        T.op("act", lambda e, hu=hu, tsl=tsl, pb=pb: e.activation(out=zsT[:, hu, tsl], in_=psa(*pb), func=AF.Silu), [psk(*pb)], [("zsT", hu, tc)])
        flush()
        T.barrier()
        if debug in (1.5, 1.6, 1.7):
            return nc
        if debug == 2:
            dtmp = view(arena, 0, [128, NOWN], F32)
            for hu in range(16):
                T.op("dve", lambda e, hu=hu: e.tensor_copy(out=dtmp, in_=qT[:, hu, :]), [], ["dtmp"])
                T.dma("sp", dbg["qT"][:, hu, :], dtmp, ["dtmp"], [], "d_dbg")
                T.op("dve", lambda e, hu=hu: e.tensor_copy(out=dtmp, in_=zsT[:, hu, :]), [], ["dtmp"])
                T.dma("sp", dbg["zs"][:, hu, :], dtmp, ["dtmp"], [], "d_dbg")
            T.barrier()
            return nc

        ogT = bufB[:].rearrange("p (k t) -> p k t", k=16)
        kvb = view(arena, 0, [128, 2, 8192], BF16)
        fa = view(arena, 32, [128, 2, CH], F32)
        fb = view(arena, 36, [128, 2, CH], F32)
        fo = view(arena, 40, [128, 2, CH], F32)
        ft = view(arena, 44, [128, 2, CH], F32)
        lnb = view(arena, 48, [128, 2, 1024], F32)
        rsb = view(big, 64, [128, 2, 1024], F32)
        sqb = view(big, 72, [128, 2, CH], BF16)
        pT = view(big, 74, [128, 3, 1024], BF16)
        units = [("A", h, h) for h in range(8)] + [("B", h, 8 + h // 4) for h in range(8)]
        loaded = {}
        nload = [0]

        def load_kv(src_):
            slot_ = nload[0] % 2
            nload[0] += 1
            T.dma("sp", kvb[:, slot_, 0:4096], kT_scr[src_], [], [("kvK", slot_)], "d_kvK%d" % slot_)
            T.dma("sp", kvb[:, slot_, 4096:8192], v_scr[src_].rearrange("p k d -> p (k d)"), [], [("kvV", slot_)], "d_kvV%d" % slot_)
            loaded[src_] = slot_

        srcs = []
        for u in units:
            if u[2] not in srcs:
                srcs.append(u[2])
        load_kv(srcs[0])
        pending = deque()
        blocks = [(ui, br, h, src_, qb) for ui, (br, h, src_) in enumerate(units) for qb in range(2)]
        NIT = len(blocks) * 32

        def blk(i):
            ui, br, h, src_, qb = blocks[i // 32]
            kt = i % 32
            hu = h if br == "A" else 8 + h
            slot = loaded[src_]
            Kt = kvb[:, slot, 0:4096]
            Vt = kvb[:, slot, 4096:8192].rearrange("p (k d) -> p k d", k=32)
            return br, hu, src_, qb, kt, slot, Kt, Vt

        def emit_S(i):
            br, hu, src_, qb, kt, slot, Kt, Vt = blk(i)
            sp_i = i % 2
            pj = i % 3
            qsl = slice(qb * CH, (qb + 1) * CH)
            ksl = slice(kt * 128, (kt + 1) * 128)
            kK = ("kvK", slot)
            qkey = ("qT", hu, qb)
            if br == "A":
                def smm(e):
                    e.matmul(psa(sp_i, 0), lhsT=Kt[0:64, ksl], rhs=qT[0:64, hu, qsl], start=True, stop=True)
                    return e.matmul(psa(sp_i, 1), lhsT=Kt[64:128, ksl], rhs=qT[64:128, hu, qsl], start=True, stop=True)
                T.op("pe", smm, [kK, qkey], [psk(sp_i, 0), psk(sp_i, 1)])
                T.op("act", lambda e: e.activation(out=pT[:, pj, :], in_=PS[sp_i][:], func=AF.Exp, scale=SC_A),
                     [psk(sp_i, 0), psk(sp_i, 1)], [("pT", pj)])
            else:
                T.op("pe", lambda e: e.matmul(psa(sp_i, 0), lhsT=Kt[:, ksl], rhs=qT[:, hu, qsl], start=True, stop=True),
                     [kK, qkey], [psk(sp_i, 0)])
                T.op("act", lambda e: e.activation(out=pT[:, pj, 0:512], in_=psa(sp_i, 0), func=AF.Exp, scale=SC_B),
                     [psk(sp_i, 0)], [("pT", pj)])

        def emit_PV(i):
            br, hu, src_, qb, kt, slot, Kt, Vt = blk(i)
            pj = i % 3
            kV = ("kvV", slot)
            if kt == 0 and qb == 0:
                si = srcs.index(src_)
                if si + 1 < len(srcs) and srcs[si + 1] not in loaded:
                    load_kv(srcs[si + 1])
            if br == "A":
                def pv(e):
                    e.matmul(psa(2, 0), lhsT=Vt[:, kt, :], rhs=pT[:, pj, 0:512], start=(kt == 0), stop=(kt == 31))
                    e.matmul(psa(2, 1), lhsT=Vt[:, kt, :], rhs=pT[:, pj, 512:1024], start=(kt == 0), stop=(kt == 31))
                    e.matmul(psa(3, 0), lhsT=ones, rhs=pT[:, pj, 0:512], start=(kt == 0), stop=(kt == 31))
                    return e.matmul(psa(3, 1), lhsT=ones, rhs=pT[:, pj, 512:1024], start=(kt == 0), stop=(kt == 31))
                T.op("pe", pv, [kV, ("pT", pj), "cm"], [psk(2, 0), psk(2, 1), psk(3, 0), psk(3, 1)])
            else:
                def pv(e):
                    e.matmul(psa(2, 0), lhsT=Vt[:, kt, :], rhs=pT[:, pj, 0:512], start=(kt == 0), stop=(kt == 31))
                    return e.matmul(psa(3, 0), lhsT=ones, rhs=pT[:, pj, 0:512], start=(kt == 0), stop=(kt == 31))
                T.op("pe", pv, [kV, ("pT", pj), "cm"], [psk(2, 0), psk(3, 0)])

        def emit_finish(i):
            br, hu, src_, qb, kt, slot, Kt, Vt = blk(i)
            qsl = slice(qb * CH, (qb + 1) * CH)
            fj = qb
            okey = ("ogT", hu, qb)
            zkey = ("zsT", hu, qb)
            while pending:
                pending.popleft()()
            if br == "A":
                T.op("act", lambda e: e.activation(out=lnb[:, fj, :], in_=PS[3][:], func=AF.Ln), [psk(3, 0), psk(3, 1)], [("lnb", fj)])
                T.op("act", lambda e: e.activation(out=rsb[:, fj, :], in_=lnb[:, fj, :], func=AF.Exp, scale=-1.0), [("lnb", fj)], [("rsb", fj)])
                T.op("dve", lambda e: e.tensor_tensor(out=fa[:, fj, :], in0=psa(2, 0), in1=rsb[:, fj, 0:512], op=ALU.mult), [psk(2, 0), ("rsb", fj)], [("fa", fj)])
                T.op("dve", lambda e: e.tensor_tensor(out=fb[:, fj, :], in0=psa(2, 1), in1=rsb[:, fj, 512:1024], op=ALU.mult), [psk(2, 1), ("rsb", fj)], [("fb", fj)])

                def s_o():
                    T.op("dve", lambda e: e.scalar_tensor_tensor(out=fo[:, fj, :], in0=fb[:, fj, :], scalar=neglam, in1=fa[:, fj, :], op0=ALU.mult, op1=ALU.add),
                         [("fa", fj), ("fb", fj)], [("fo", fj)])

                def s_sq():
                    T.op("act", lambda e: e.activation(out=sqb[:, fj, :], in_=fo[:, fj, :], func=AF.Square), [("fo", fj)], [("sqb", fj)])

                def s_mmln():
                    T.op("pe", lambda e: e.matmul(psa(fj, 0), lhsT=ones, rhs=sqb[:, fj, :], start=True, stop=True), [("sqb", fj), "cm"], [psk(fj, 0)])
                    T.op("act", lambda e: e.activation(out=lnb[:, fj, 0:CH], in_=psa(fj, 0), func=AF.Ln, scale=1.0 / 128, bias=epsb[:]), [psk(fj, 0)], [("lnb", fj)])

                def s_ex():
                    T.op("act", lambda e: e.activation(out=rsb[:, fj, 0:CH], in_=lnb[:, fj, 0:CH], func=AF.Exp, scale=-0.5), [("lnb", fj)], [("rsb", fj)])

                def s_g():
                    T.op("dve", lambda e: e.scalar_tensor_tensor(out=ft[:, fj, :], in0=fo[:, fj, :], scalar=gsub, in1=rsb[:, fj, 0:CH], op0=ALU.mult, op1=ALU.mult),
                         [("fo", fj), ("rsb", fj)], [("ft", fj)])

                def s_z():
                    T.op("pool", lambda e: e.tensor_tensor(out=ogT[:, hu, qsl], in0=ft[:, fj, :], in1=zsT[:, hu, qsl], op=ALU.mult), [("ft", fj), zkey], [okey])
                pending.extend([s_o, s_sq, s_mmln, s_ex, s_g, s_z])
            else:
                T.op("act", lambda e: e.activation(out=lnb[:, fj, 0:CH], in_=psa(3, 0), func=AF.Ln), [psk(3, 0)], [("lnb", fj)])
                T.op("act", lambda e: e.activation(out=rsb[:, fj, 0:CH], in_=lnb[:, fj, 0:CH], func=AF.Exp, scale=-1.0), [("lnb", fj)], [("rsb", fj)])
                T.op("dve", lambda e: e.tensor_tensor(out=ft[:, fj, :], in0=psa(2, 0), in1=rsb[:, fj, 0:CH], op=ALU.mult), [psk(2, 0), ("rsb", fj)], [("ft", fj)])

                def s_z():
                    T.op("pool", lambda e: e.tensor_tensor(out=ogT[:, hu, qsl], in0=ft[:, fj, :], in1=zsT[:, hu, qsl], op=ALU.mult), [("ft", fj), zkey], [okey])
                pending.extend([s_z])

        emit_S(0)
        for i in range(NIT):
            if i + 1 < NIT:
                nsrc = blocks[(i + 1) // 32][3]
                if nsrc not in loaded:
                    load_kv(nsrc)
                emit_S(i + 1)
            emit_PV(i)
            if i % 32 == 31:
                emit_finish(i)
            elif pending and i % 32 >= 2:
                pending.popleft()()
        while pending:
            pending.popleft()()
        T.barrier()
        if debug == 3:
            dtmp = view(arena, 0, [128, NOWN], F32)
            for hu in range(16):
                T.op("dve", lambda e, hu=hu: e.tensor_copy(out=dtmp, in_=ogT[:, hu, :]), [], ["dtmp"])
                T.dma("sp", dbg["og"][:, hu, :], dtmp, ["dtmp"], [], "d_dbg")
            T.barrier()
            return nc

        mT = view(arena, 0, [128, 16, NOWN], BF16)
        gsl = [[view(big, (s * 2 + g) * 8, [128, KC, 256], BF16) for g in range(2)] for s in range(2)]
        wo_ab = view(big, 32, [128, 2, 2, 8, 256], BF16)
        lnb = view(big, 48, [128, 2, 1024], F32)
        rsb = view(big, 56, [128, 2, 1024], F32)

        def load_p4(db):
            s_ = db % 2
            load_wblock(gsl[s_][0], w_in, C_GA + db * 256, 256, ("gsl", s_, 0), "d_gsl%d0" % s_)
            load_wblock(gsl[s_][1], w_in, C_GB + db * 256, 256, ("gsl", s_, 1), "d_gsl%d1" % s_)
            T.dma("pool", wo_ab[:, s_, 0], w_oa[:, db * 256:(db + 1) * 256].rearrange("(f p) n -> p f n", p=128), [], [("woab", s_, 0)], "d_woa%d" % s_)
            T.dma("pool", wo_ab[:, s_, 1], w_ob[:, db * 256:(db + 1) * 256].rearrange("(f p) n -> p f n", p=128), [], [("woab", s_, 1)], "d_wob%d" % s_)

        load_p4(0)
        for db in range(8):
            s = db % 2
            if db + 1 < 8:
                load_p4(db + 1)
            for c2 in range(2):
                dc = db * 2 + c2
                csl = slice(c2 * 128, (c2 + 1) * 128)
                for tc in range(2):
                    tsl = slice(tc * CH, (tc + 1) * CH)
                    hk = [("hT_own", 0), ("hT_own", 1)]
                    pg, py = 2 * tc, 2 * tc + 1
                    proj_fm((pg, 0), lambda kc: gsl[s][0][:, kc, csl], lambda kc: hT_own[:, kc, tsl], hk + [("gsl", s, 0)])
                    proj_fm((pg, 1), lambda kc: gsl[s][1][:, kc, csl], lambda kc: hT_own[:, kc, tsl], hk + [("gsl", s, 1)])

                    def ya(e, br_, pb):
                        last = None
                        for fc in range(8):
                            last = e.matmul(psa(*pb), lhsT=wo_ab[:, s, br_, fc, csl], rhs=ogT[:, br_ * 8 + fc, tsl], start=(fc == 0), stop=(fc == 7))
                        return last
                    ogk = [("ogT", hh, tc) for hh in range(16)]
                    T.op("pe", lambda e: ya(e, 0, (py, 0)), ogk + [("woab", s, 0)], [psk(py, 0)])
                    T.op("pe", lambda e: ya(e, 1, (py, 1)), ogk + [("woab", s, 1)], [psk(py, 1)])
                    j = tc
                    T.op("act", lambda e: e.activation(out=lnb[:, j, :], in_=PS[pg][:], func=AF.Sigmoid), [psk(pg, 0), psk(pg, 1)], [("lnb", j)])
                    T.op("dve", lambda e: e.tensor_tensor(out=rsb[:, j, :], in0=PS[py][:], in1=lnb[:, j, :], op=ALU.mult), [psk(py, 0), psk(py, 1), ("lnb", j)], [("rsb", j)])
                    T.op("pool", lambda e: e.tensor_tensor(out=mT[:, dc, tsl], in0=rsb[:, j, 0:512], in1=rsb[:, j, 512:1024], op=ALU.add), [("rsb", j)], [("mT", dc, tc)])
        T.barrier()
        if debug == 4:
            dtmp = view(big, 0, [128, NOWN], F32)
            for hu in range(16):
                T.op("dve", lambda e, hu=hu: e.tensor_copy(out=dtmp, in_=mT[:, hu, :]), [], ["dtmp"])
                T.dma("sp", dbg["mg"][:, hu, :], dtmp, ["dtmp"], [], "d_dbg")
            T.barrier()
            return nc

        wo_sb = big[:, 0:32768].rearrange("p (k n) -> p k n", k=KC)
        junk5 = view(big, 64, [128, D], BF16)
        xs5 = view(arena, 32, [128, 2, D], F32)
        gbc = view(arena, 48, [128, D], F32)
        yb = bufB[:, 0:8192].bitcast(F32).rearrange("p (s n) -> p s n", s=2)
        for cb in range(4):
            T.dma("pool", wo_sb[:, :, cb * 512:(cb + 1) * 512], w_o[:, cb * 512:(cb + 1) * 512].rearrange("(k p) n -> p k n", p=128), [], [("wo", cb)], "d_wo%d" % cb)
        T.dma("sp", gbc, final_g.partition_broadcast(128), [], ["gbc"], "d_gbc")
        for tt in range(8):
            xb = tt % 2
            xk = ("xs", xb)
            T.dma("sp", xs5[:, xb, :], xkv[tt * 128:(tt + 1) * 128, :], [], [xk], "d_xs%d" % xb)
            yk = ("yb", xb)
            for cb in range(4):
                pb = (cb // 2, cb % 2) if tt % 2 == 0 else (2 + cb // 2, cb % 2)

                def f(e):
                    last = None
                    for kc in range(KC):
                        last = e.matmul(psa(*pb), lhsT=mT[:, kc, tt * 128:(tt + 1) * 128], rhs=wo_sb[:, kc, cb * 512:(cb + 1) * 512], start=(kc == 0), stop=(kc == KC - 1))
                    return last
                T.op("pe", f, [("mT", kc, tt // 4) for kc in range(KC)] + [("wo", cb)], [psk(*pb)])
                T.op("dve", lambda e: e.tensor_tensor(out=yb[:, xb, cb * 512:(cb + 1) * 512], in0=psa(*pb), in1=xs5[:, xb, cb * 512:(cb + 1) * 512], op=ALU.add),
                     [psk(*pb), xk], [yk])
            rstd, sk = rms_stats(yb[:, xb, :], yk, xb, junk5, "junk5")
            T.op("dve", lambda e: e.scalar_tensor_tensor(out=xs5[:, xb, :], in0=yb[:, xb, :], scalar=rstd, in1=gbc, op0=ALU.mult, op1=ALU.mult), [yk, sk, "gbc"], [xk])
            T.dma("sp", out[tt * 128:(tt + 1) * 128, :], xs5[:, xb, :], [xk], [], "d_out%d" % xb)
        T.barrier()
    return nc


def _rope_tables():
    inv = (1.0 / (10000.0 ** (np.arange(0, 64, 2, dtype=np.float32) / np.float32(64)))).astype(np.float32)
    pos = np.arange(S, dtype=np.float32)
    angA = pos[:, None] * inv[None, :]
    angR = (np.arange(S) // 64).astype(np.float32)[:, None] * inv[None, :]
    angC = (np.arange(S) % 64).astype(np.float32)[:, None] * inv[None, :]
    p = np.arange(128)
    f = p % 32
    sign = np.where((p % 64) < 32, -1.0, 1.0).astype(np.float32)
    cosA = np.cos(angA)[:, f].T.astype(np.float32)
    sinA = (np.sin(angA)[:, f].T * sign[:, None]).astype(np.float32)
    angB = np.where((p < 64)[None, :], angR[:, f], angC[:, f])
    cosB = np.cos(angB).T.astype(np.float32)
    sinB = (np.sin(angB).T * sign[:, None]).astype(np.float32)
    return cosA, sinA, cosB, sinB


def _const_mats():
    ident = np.eye(128, dtype=np.float32)
    p = np.arange(128)
    partner = np.where((p % 64) < 32, p + 32, p - 32)
    perm = np.zeros((128, 128), np.float32)
    perm[partner, p] = 1.0
    ones = np.ones((128, 128), np.float32)
    return np.ascontiguousarray(np.stack([ident, perm, ones], axis=1))


def make_in_maps(x, norm_g, w_in, lambda_q1, lambda_k1, lambda_q2, lambda_k2, subln_g, q_norm_g, k_norm_g,
                 w_out_a, w_out_b, w_o, final_g):
    f32 = lambda a: np.ascontiguousarray(np.asarray(a, dtype=np.float32))
    x = f32(x)
    tabs = _rope_tables()
    cm = _const_mats()
    lam4 = f32(np.concatenate([np.asarray(lambda_q1)[0], np.asarray(lambda_k1)[0], np.asarray(lambda_q2)[0], np.asarray(lambda_k2)[0]]))
    gvecs = f32(np.stack([np.asarray(subln_g)[0], np.asarray(q_norm_g)[0], np.asarray(k_norm_g)[0]], axis=1))
    w_in0, woa0, wob0, wo0 = f32(np.asarray(w_in)[0]), f32(np.asarray(w_out_a)[0]), f32(np.asarray(w_out_b)[0]), f32(np.asarray(w_o)[0])
    ng, fg = f32(np.asarray(norm_g)[0]), f32(final_g)
    maps = []
    for core in range(8):
        b, c = core // 4, core % 4
        order = np.roll(np.arange(S), -c * NOWN)
        m = {"xkv": np.ascontiguousarray(x[b][order]), "w_in": w_in0, "w_out_a": woa0, "w_out_b": wob0, "w_o": wo0,
             "norm_g": ng, "final_g": fg, "lam4": lam4, "gvecs": gvecs, "cmats": cm}
        for n, t in zip(("cosA", "sinA", "cosB", "sinB"), tabs):
            m[n] = np.ascontiguousarray(t[:, order])
        maps.append(m)
    return maps


def kernel(**inputs):
    nc = build_nc()
    maps = make_in_maps(**inputs)
    res = run_bass_kernel_spmd(nc, maps, core_ids=list(range(8)))
    out = np.empty((2, S, D), np.float32)
    for core in range(8):
        b, c = core // 4, core % 4
        out[b, c * NOWN:(c + 1) * NOWN] = res.results[core]["out"]
    return out
```

```python
import math
from collections import deque
from contextlib import ExitStack

import numpy as np
import concourse.bass as bass
import concourse.mybir as mybir
from concourse.bass_utils import run_bass_kernel_spmd

F32 = mybir.dt.float32
BF16 = mybir.dt.bfloat16
AF = mybir.ActivationFunctionType
ALU = mybir.AluOpType

D = 2048
KC = 16
S = 4096
NOWN = 1024
CH = 512
NCH = S // CH
COLS = 10752
EPS = 1e-6
LAM_INIT = 0.8 - 0.6 * math.exp(-0.3 * 0)
C_QA, C_KA, C_VA, C_ZA, C_QB, C_KB, C_VB, C_ZB, C_GA, C_GB = 0, 1024, 2048, 3072, 4096, 5120, 5376, 5632, 6656, 8704
SC_A = 64 ** -0.5
SC_B = 128 ** -0.5


class Tracker:
    def __init__(self, nc, stack):
        self.nc = nc
        self.stack = stack
        self.eng = {"pe": nc.tensor, "act": nc.scalar, "dve": nc.vector, "pool": nc.gpsimd, "sp": nc.sync}
        self.sems = {}
        self.cnt = {}
        self.waited = {e: {} for e in self.eng}
        self.lastw = {}
        self.readers = {}
        for e in self.eng:
            self._sem("e_" + e)

    def _sem(self, name):
        if name not in self.sems:
            self.sems[name] = self.stack.enter_context(self.nc.semaphore(name))
            self.cnt[name] = 0
        return self.sems[name]

    def _wait_for(self, e, toks):
        need = {}
        for t in toks:
            if t is None:
                continue
            s, v = t
            if v > need.get(s, 0):
                need[s] = v
        for s, v in need.items():
            if self.waited[e].get(s, 0) >= v:
                continue
            if e == "pe" and s == "e_pe":
                continue
            self.eng[e].wait_ge(self.sems[s], v)
            self.waited[e][s] = v

    def _deps(self, reads, writes, e=None):
        toks = []
        for k in reads:
            toks.append(self.lastw.get(k))
        own = "e_" + e if e else None
        for k in writes:
            for t in [self.lastw.get(k)] + list(self.readers.get(k, ())):
                if t is not None and t[0] != own:
                    toks.append(t)
        return toks

    def _record(self, tok, reads, writes):
        for k in reads:
            self.readers.setdefault(k, []).append(tok)
        for k in writes:
            self.lastw[k] = tok
            self.readers[k] = []

    @staticmethod
    def _excl(reads, writes):
        ps_reads = [k for k in reads if isinstance(k, tuple) and k and k[0] == "ps"]
        if not ps_reads:
            return reads, writes
        return [k for k in reads if k not in ps_reads], list(writes) + ps_reads

    def op(self, e, fn, reads=(), writes=()):
        reads, writes = self._excl(reads, writes)
        self._wait_for(e, self._deps(reads, writes, e))
        ins = fn(self.eng[e])
        s = "e_" + e
        self.cnt[s] += 1
        ins.then_inc(self.sems[s], 1)
        self._record((s, self.cnt[s]), reads, writes)

    def dma(self, q, out, in_, reads, writes, sem):
        self._sem(sem)
        self._wait_for(q, self._deps(reads, writes))
        ins = self.eng[q].dma_start(out=out, in_=in_)
        self.cnt[sem] += 16
        ins.then_inc(self.sems[sem], 16)
        self._record((sem, self.cnt[sem]), reads, writes)

    def barrier(self):
        toks = [(s, c) for s, c in self.cnt.items() if c > 0]
        for e in self.eng:
            self._wait_for(e, toks)
        self.lastw.clear()
        self.readers.clear()


def build_nc(debug=None):
    nc = bass.Bass("TRN2", target_bir_lowering=False)
    dt = nc.dram_tensor
    xkv = dt("xkv", [S, D], F32, kind="ExternalInput").ap()
    w_in = dt("w_in", [D, COLS], F32, kind="ExternalInput").ap()
    w_oa = dt("w_out_a", [1024, D], F32, kind="ExternalInput").ap()
    w_ob = dt("w_out_b", [1024, D], F32, kind="ExternalInput").ap()
    w_o = dt("w_o", [D, D], F32, kind="ExternalInput").ap()
    norm_g = dt("norm_g", [D], F32, kind="ExternalInput").ap()
    final_g = dt("final_g", [D], F32, kind="ExternalInput").ap()
    lam4 = dt("lam4", [256], F32, kind="ExternalInput").ap()
    gvecs = dt("gvecs", [128, 3], F32, kind="ExternalInput").ap()
    tabs = {n: dt(n, [128, S], F32, kind="ExternalInput").ap() for n in ("cosA", "sinA", "cosB", "sinB")}
    cmats = dt("cmats", [128, 3, 128], F32, kind="ExternalInput").ap()
    out = dt("out", [NOWN, D], F32, kind="ExternalOutput").ap()
    kT_scr = dt("kT_scr", [10, 128, S], BF16, kind="Internal").ap()
    v_scr = dt("v_scr", [10, 128, 32, 128], BF16, kind="Internal").ap()
    dbg = {}
    if debug:
        dbg["qT"] = dt("dbg_qT", [128, 16, NOWN], F32, kind="ExternalOutput").ap()
        dbg["zs"] = dt("dbg_zs", [128, 16, NOWN], F32, kind="ExternalOutput").ap()
        dbg["kT"] = dt("dbg_kT", [10, 128, S], BF16, kind="ExternalOutput").ap()
        dbg["v"] = dt("dbg_v", [10, 128, 32, 128], BF16, kind="ExternalOutput").ap()
        dbg["og"] = dt("dbg_og", [128, 16, NOWN], F32, kind="ExternalOutput").ap()
        dbg["mg"] = dt("dbg_mg", [128, 16, NOWN], F32, kind="ExternalOutput").ap()

    with ExitStack() as st:
        E = st.enter_context
        T = Tracker(nc, st)
        sb = lambda name, shape, dtp: E(nc.sbuf_tensor(name, shape, dtp))

        big = sb("big", [128, 40960], BF16)
        hT_own = sb("hT_own", [128, KC, NOWN], BF16)
        bufB = sb("bufB", [128, 16384], BF16)
        arena = sb("arena", [128, 29184], BF16)
        cm = sb("cm", [128, 3, 128], BF16)
        gv = sb("gv", [128, 3], F32)
        g16 = sb("g16", [128, KC], F32)
        lamb = sb("lamb", [128, 256], F32)
        lamt = sb("lamt", [128, 8], F32)
        epsb = sb("epsb", [128, 1], F32)
        st8 = sb("st8", [128, 16], F32)

        def view(buf, off_kib, shape, dtp):
            esz = 4 if dtp == F32 else 2
            n = 1
            for s_ in shape[1:]:
                n *= s_
            a_ = int(off_kib * 1024) // 2
            ap = buf[:, a_:a_ + n * esz // 2]
            if dtp == F32:
                ap = ap.bitcast(F32)
            if len(shape) == 3:
                ap = ap.rearrange("p (a b) -> p a b", a=shape[1])
            elif len(shape) == 4:
                ap = ap.rearrange("p (a b c) -> p a b c", a=shape[1], b=shape[2])
            elif len(shape) == 5:
                ap = ap.rearrange("p (a b c d) -> p a b c d", a=shape[1], b=shape[2], c=shape[3])
            return ap

        xs = view(arena, 0, [128, 2, D], F32)
        hb2 = view(arena, 16, [128, 2, D], BF16)
        tab = view(arena, 24, [128, 4, CH], F32)
        kb = view(arena, 32, [128, 2, CH], BF16)
        t1 = view(arena, 34, [128, 2, CH], F32)
        t2 = view(arena, 38, [128, 2, CH], F32)
        lnb = view(arena, 42, [128, 2, CH], F32)
        rsb = view(arena, 46, [128, 2, CH], F32)
        sqb = view(arena, 50, [128, 2, CH], BF16)
        kout = view(arena, 52, [128, 2, CH], BF16)
        vout = view(arena, 54, [128, 1280], BF16)
        PS = [E(nc.psum_tensor("ps%d" % i, [128, 1024], F32)) for i in range(4)]

        ident = cm[:, 0, :]
        perm = cm[:, 1, :]
        ones = cm[:, 2, :]

        def psk(i, h):
            return ("ps", i, h)

        def psa(i, h):
            return PS[i][:, h * 512:(h + 1) * 512]

        T.dma("pool", cm[:], cmats, [], ["cm"], "d_cm")
        T.dma("sp", gv[:], gvecs, [], ["gv"], "d_gv")
        T.dma("sp", lamb[:], lam4.partition_broadcast(128), [], ["lamb"], "d_lam")
        with nc.allow_non_contiguous_dma(reason="tiny gain vector transpose load"):
            T.dma("sp", g16[:], norm_g.rearrange("(k p) -> p k", p=128), [], ["g16"], "d_g16")
        T.op("dve", lambda e: e.memset(epsb[:], EPS), [], ["epsb"])
        T.op("dve", lambda e: e.tensor_tensor(out=t1[:, 0, 0:64], in0=lamb[:, 0:64], in1=lamb[:, 64:128], op=ALU.mult), ["lamb"], ["t1c"])
        T.op("dve", lambda e: e.tensor_tensor(out=t1[:, 0, 64:128], in0=lamb[:, 128:192], in1=lamb[:, 192:256], op=ALU.mult), ["lamb", "t1c"], ["t1c"])
        T.op("dve", lambda e: e.reduce_sum(out=lamt[:, 0:1], in_=t1[:, 0, 0:64], axis=mybir.AxisListType.X), ["t1c"], ["lamt"])
        T.op("dve", lambda e: e.reduce_sum(out=lamt[:, 1:2], in_=t1[:, 0, 64:128], axis=mybir.AxisListType.X), ["t1c", "lamt"], ["lamt"])
        T.op("act", lambda e: e.activation(out=lamt[:, 2:4], in_=lamt[:, 0:2], func=AF.Exp), ["lamt"], ["lamt"])
        T.op("dve", lambda e: e.scalar_tensor_tensor(out=lamt[:, 4:5], in0=lamt[:, 3:4], scalar=-LAM_INIT, in1=lamt[:, 2:3], op0=ALU.add, op1=ALU.subtract), ["lamt"], ["lamt"])
        T.op("dve", lambda e: e.tensor_scalar(out=lamt[:, 5:6], in0=gv[:, 0:1], scalar1=(1.0 - LAM_INIT), scalar2=None, op0=ALU.mult), ["gv", "lamt"], ["lamt"])
        neglam = lamt[:, 4:5]
        gsub = lamt[:, 5:6]
        T.barrier()
        if debug == 0.1:
            return nc

        def load_wblock(dst3, src_w, col0, ncols, key, sem):
            T.dma("pool", dst3, src_w[:, col0:col0 + ncols].rearrange("(kc p) n -> p kc n", p=128), [], [key], sem)

        def rope_finish(src_ap, src_keys, j, cos_ap, sin_ap, tab_keys, out_ap, out_key, rot_ps, inplace=False):
            ri, rh = rot_ps
            T.op("act", lambda e: e.activation(out=kb[:, j, :], in_=src_ap, func=AF.Copy), src_keys, [("kb", j)])
            T.op("pe", lambda e: e.matmul(psa(ri, rh), lhsT=perm, rhs=kb[:, j, :], start=True, stop=True), [("kb", j), "cm"], [psk(ri, rh)])
            T.op("dve", lambda e: e.tensor_tensor(out=t1[:, j, :], in0=src_ap, in1=cos_ap, op=ALU.mult), list(src_keys) + list(tab_keys), [("t1", j)])
            T.op("dve", lambda e: e.tensor_tensor(out=t2[:, j, :], in0=psa(ri, rh), in1=sin_ap, op=ALU.mult), [psk(ri, rh)] + list(tab_keys), [("t2", j)])
            T.op("pool", lambda e: e.tensor_tensor(out=out_ap, in0=t1[:, j, :], in1=t2[:, j, :], op=ALU.add), [("t1", j), ("t2", j)], [out_key])

        def headnorm(src_ps, src_key, j, gcol, aux_ps):
            ai, ah = aux_ps
            T.op("act", lambda e: e.activation(out=sqb[:, j, :], in_=src_ps, func=AF.Square), [src_key], [("sqb", j)])
            T.op("pe", lambda e: e.matmul(psa(ai, ah), lhsT=ones, rhs=sqb[:, j, :], start=True, stop=True), [("sqb", j), "cm"], [psk(ai, ah)])
            T.op("act", lambda e: e.activation(out=lnb[:, j, 0:CH], in_=psa(ai, ah), func=AF.Ln, scale=1.0 / 128, bias=epsb[:]), [psk(ai, ah), "epsb"], [("lnb", j)])
            T.op("act", lambda e: e.activation(out=rsb[:, j, 0:CH], in_=lnb[:, j, 0:CH], func=AF.Exp, scale=-0.5), [("lnb", j)], [("rsb", j)])
            T.op("dve", lambda e: e.scalar_tensor_tensor(out=t1[:, j, :], in0=src_ps, scalar=gv[:, gcol:gcol + 1], in1=rsb[:, j, 0:CH], op0=ALU.mult, op1=ALU.mult), [src_key, ("rsb", j), "gv"], [("t1", j)])

        def proj_fm(ps_idx, wfun, rfun, reads):
            pi, ph = ps_idx

            def f(e):
                last = None
                for kc in range(KC):
                    last = e.matmul(psa(pi, ph), lhsT=wfun(kc), rhs=rfun(kc), start=(kc == 0), stop=(kc == KC - 1))
                return last
            T.op("pe", f, reads, [psk(pi, ph)])

        def rms_stats(src_ap, src_key, slot, junk_ap, junk_key):
            sc = st8[:, 4 * slot:4 * slot + 4]
            sk = ("st8", slot)
            T.op("act", lambda e: e.activation(out=junk_ap, in_=src_ap, func=AF.Square, accum_out=sc[:, 0:1]), [src_key], [junk_key, sk])
            T.op("act", lambda e: e.activation(out=sc[:, 1:2], in_=sc[:, 0:1], func=AF.Ln, scale=1.0 / D, bias=epsb[:]), [sk, "epsb"], [sk])
            T.op("act", lambda e: e.activation(out=sc[:, 2:3], in_=sc[:, 1:2], func=AF.Exp, scale=-0.5), [sk], [sk])
            return sc[:, 2:3], sk

        dq = []
        tickc = [0]

        def defer(n, fn):
            dq.append([tickc[0] + n, fn])

        def tick():
            tickc[0] += 1
            for d_ in [d_ for d_ in dq if d_[0] <= tickc[0]]:
                dq.remove(d_)
                d_[1]()

        def flush():
            while dq:
                dq.pop(0)[1]()

        wkv = big[:, 0:KC * 2560].rearrange("p (k n) -> p k n", k=KC)
        wkv_blocks = [(0, C_KA, 512), (512, C_KA + 512, 512), (1024, C_KB, 256),
                      (1280, C_VA, 512), (1792, C_VA + 512, 512), (2304, C_VB, 256)]
        for bi, (dcol, scol, n) in enumerate(wkv_blocks):
            load_wblock(wkv[:, :, dcol:dcol + n], w_in, scol, n, ("wkv", bi), "d_wkv%d" % bi)
        wkv_keys = [("wkv", bi) for bi in range(6)]
        if debug == 0.2:
            T.barrier()
            return nc
        hT_rot = bufB[:].rearrange("p (b k t) -> p b k t", b=2, k=KC)
        kcount = 0

        def tile_dst(ch, t4):
            if ch < 2:
                return hT_own, ("hT_own", ch), ch * CH + t4 * 128
            return hT_rot[:, ch % 2], ("hT_rot", ch % 2), t4 * 128

        def prepX(tt):
            if tt >= 32:
                return
            xbuf = tt % 2
            T.dma("sp", xs[:, xbuf, :], xkv[tt * 128:(tt + 1) * 128, :], [], [("xs", xbuf)], "d_xs%d" % xbuf)

        def prepA(ch, t4):
            tt = ch * 4 + t4
            xbuf = tt % 2
            xk = ("xs", xbuf)
            rstd, sk = rms_stats(xs[:, xbuf, :], xk, xbuf, hb2[:, xbuf, :], ("hb", xbuf))
            T.op("act", lambda e: e.activation(out=hb2[:, xbuf, :], in_=xs[:, xbuf, :], func=AF.Copy, scale=rstd), [xk, sk], [("hb", xbuf)])
            prepX(tt + 2)

        def prepB(ch, t4):
            tt = ch * 4 + t4
            xbuf = tt % 2
            dstT, dkey, tok_off = tile_dst(ch, t4)
            pst = PS[3][:].bitcast(BF16)

            def tr(e):
                last = None
                for kc in range(KC):
                    last = e.transpose(pst[:, kc * 128:(kc + 1) * 128], hb2[:, xbuf, kc * 128:(kc + 1) * 128], ident)
                return last
            T.op("pe", tr, [("hb", xbuf), "cm"], [psk(3, 0), psk(3, 1)])
            for kc in range(KC):
                T.op("dve", lambda e, kc=kc: e.tensor_scalar(out=dstT[:, kc, tok_off:tok_off + 128], in0=pst[:, kc * 128:(kc + 1) * 128],
                                                             scalar1=g16[:, kc:kc + 1], scalar2=None, op0=ALU.mult),
                     [psk(3, 0), psk(3, 1), "g16"], [dkey])

        def load_tabs(ch, which):
            for ti in which:
                tn = ("cosA", "sinA", "cosB", "sinB")[ti]
                T.dma("sp", tab[:, ti, :], tabs[tn][:, ch * CH:(ch + 1) * CH], [], [("tab", ti)], "d_tab%d" % ti)

        load_tabs(0, (0, 1, 2, 3))
        prepX(0)
        prepX(1)
        if debug == 0.31:
            prepA(0, 0)
            T.barrier()
            return nc
        for t4 in range(4):
            prepA(0, t4)
            prepB(0, t4)
        if debug == 0.3:
            T.barrier()
            return nc
        for ch in range(NCH):
            if ch < 2:
                hT = hT_own[:, :, ch * CH:(ch + 1) * CH]
                hkey = ("hT_own", ch)
            else:
                hT = hT_rot[:, ch % 2, :, :]
                hkey = ("hT_rot", ch % 2)
            for cc in range(10):
                pb = (cc % 2, 0)
                proj_fm(pb, lambda kc, cc=cc: wkv[:, kc, cc * 128:(cc + 1) * 128], lambda kc: hT[:, kc, :], [hkey] + wkv_keys)
                tick()
                if ch + 1 < NCH and cc % 2 == 0:
                    if cc // 2 < 4:
                        prepA(ch + 1, cc // 2)
                    if 1 <= cc // 2 <= 4:
                        prepB(ch + 1, cc // 2 - 1)
                j = cc % 2
                ko = kcount % 2
                kcount += 1

                def store(cc=cc, ko=ko, ch=ch):
                    T.dma("pool", kT_scr[cc, :, ch * CH:(ch + 1) * CH], kout[:, ko, :], [("kout", ko)], [], "d_kout%d" % ko)

                if cc < 8:
                    def postA(cc=cc, pb=pb, j=j, ko=ko, ch=ch, store=store):
                        rope_finish(psa(*pb), [psk(*pb)], j, tab[:, 0, :], tab[:, 1, :], [("tab", 0), ("tab", 1)], kout[:, ko, :], ("kout", ko), (cc % 2, 1))
                        store()
                        if cc == 7 and ch + 1 < NCH:
                            load_tabs(ch + 1, (0, 1))
                    defer(1, postA)
                else:
                    def postB1(cc=cc, pb=pb, j=j):
                        headnorm(psa(*pb), psk(*pb), j, 2, (cc % 2, 1))

                    def postB2(cc=cc, pb=pb, j=j, ko=ko, ch=ch, store=store):
                        rope_finish(t1[:, j, :], [("t1", j)], j, tab[:, 2, :], tab[:, 3, :], [("tab", 2), ("tab", 3)], kout[:, ko, :], ("kout", ko), (cc % 2, 1))
                        store()
                        if cc == 9 and ch + 1 < NCH:
                            load_tabs(ch + 1, (2, 3))
                    defer(1, postB1)
                    defer(2, postB2)
            for t4 in range(4):
                tt = ch * 4 + t4
                vkey = "vout"
                for blk, (c0, n) in enumerate(((1280, 512), (1792, 512), (2304, 256))):
                    pb = (2, blk % 2)

                    def f(e, c0=c0, n=n, pb=pb, t4=t4):
                        last = None
                        for kc in range(KC):
                            last = e.matmul(psa(*pb)[:, 0:n], lhsT=hT[:, kc, t4 * 128:(t4 + 1) * 128], rhs=wkv[:, kc, c0:c0 + n], start=(kc == 0), stop=(kc == KC - 1))
                        return last
                    T.op("pe", f, [hkey] + wkv_keys, [psk(*pb)])
                    tick()
                    T.op("act", lambda e, c0=c0, n=n, pb=pb: e.activation(out=vout[:, c0 - 1280:c0 - 1280 + n], in_=psa(*pb)[:, 0:n], func=AF.Copy), [psk(*pb)], [vkey])
                T.dma("act", v_scr[:, :, tt, :].rearrange("h p d -> p h d"), vout[:].rearrange("p (h d) -> p h d", h=10), [vkey], [], "d_vout")
        flush()
        T.barrier()

        if debug == 1:
            T.dma("sp", dbg["kT"], kT_scr, [], [], "d_dbg")
            T.dma("sp", dbg["v"], v_scr, [], [], "d_dbg")
            T.barrier()
            return nc

        qT = big[:, 0:16384].rearrange("p (k t) -> p k t", k=16)
        zsT = big[:, 16384:32768].rearrange("p (k t) -> p k t", k=16)
        tab2 = view(big, 64, [128, 4, NOWN], F32)
        wst3 = [bufB[:, s * 8192:(s + 1) * 8192].rearrange("p (k n) -> p k n", k=KC) for s in range(2)]
        q_blocks = [("qA", C_QA), ("qA", C_QA + 512), ("zA", C_ZA), ("zA", C_ZA + 512),
                    ("qB", C_QB), ("qB", C_QB + 512), ("zB", C_ZB), ("zB", C_ZB + 512)]
        if debug in (1.5, 1.6, 1.7):
            q_blocks = {1.5: q_blocks[0:1], 1.6: q_blocks[2:3], 1.7: q_blocks[4:5]}[debug]
        load_wblock(wst3[0], w_in, q_blocks[0][1], 512, ("wst", 0), "d_wst0")
        for ti, tn in enumerate(("cosA", "sinA", "cosB", "sinB")):
            T.dma("sp", tab2[:, ti, :], tabs[tn][:, 0:NOWN], [], [("tab2", ti)], "d_tab%d" % ti)
        cnt = 0
        for bi, (kind, col0) in enumerate(q_blocks):
            slot = bi % 2
            for c4 in range(4):
                hu_local = (bi % 2) * 4 + c4
                for tc in range(2):
                    if c4 == 1 and tc == 0 and bi + 1 < len(q_blocks):
                        load_wblock(wst3[(bi + 1) % 2], w_in, q_blocks[bi + 1][1], 512, ("wst", (bi + 1) % 2), "d_wst%d" % ((bi + 1) % 2))
                    pb = (cnt % 2, 0)
                    aux = (cnt % 2, 1)
                    j = cnt % 2
                    cnt += 1
                    proj_fm(pb, lambda kc, c4=c4, slot=slot: wst3[slot][:, kc, c4 * 128:(c4 + 1) * 128],
                            lambda kc, tc=tc: hT_own[:, kc, tc * CH:(tc + 1) * CH], [("hT_own", 0), ("hT_own", 1), ("wst", slot)])
                    tsl = slice(tc * CH, (tc + 1) * CH)
                    tick()
                    if kind == "qA":
                        def postA(pb=pb, j=j, tsl=tsl, hu_local=hu_local, tc=tc, aux=aux):
                            rope_finish(psa(*pb), [psk(*pb)], j, tab2[:, 0, tsl], tab2[:, 1, tsl], [("tab2", 0), ("tab2", 1)], qT[:, hu_local, tsl], ("qT", hu_local, tc), aux)
                        defer(1, postA)
                    elif kind == "qB":
                        def postB1(pb=pb, j=j, aux=aux):
                            headnorm(psa(*pb), psk(*pb), j, 1, aux)

                        def postB2(pb=pb, j=j, tsl=tsl, hu_local=hu_local, tc=tc, aux=aux):
                            rope_finish(t1[:, j, :], [("t1", j)], j, tab2[:, 2, tsl], tab2[:, 3, tsl], [("tab2", 2), ("tab2", 3)], qT[:, 8 + hu_local, tsl], ("qT", 8 + hu_local, tc), aux)
                        defer(1, postB1)
                        defer(2, postB2)
                    else:
                        hu = hu_local + (0 if kind == "zA" else 8)
                        T.op("act", lambda e, hu=hu, tsl=tsl, pb=pb: e.activation(out=zsT[:, hu, tsl], in_=psa(*pb), func=AF.Silu), [psk(*pb)], [("zsT", hu, tc)])
        flush()
        T.barrier()
        if debug in (1.5, 1.6, 1.7):
            return nc
        if debug == 2:
            dtmp = view(arena, 0, [128, NOWN], F32)
            for hu in range(16):
                T.op("dve", lambda e, hu=hu: e.tensor_copy(out=dtmp, in_=qT[:, hu, :]), [], ["dtmp"])
                T.dma("sp", dbg["qT"][:, hu, :], dtmp, ["dtmp"], [], "d_dbg")
                T.op("dve", lambda e, hu=hu: e.tensor_copy(out=dtmp, in_=zsT[:, hu, :]), [], ["dtmp"])
                T.dma("sp", dbg["zs"][:, hu, :], dtmp, ["dtmp"], [], "d_dbg")
            T.barrier()
            return nc

        ogT = bufB[:].rearrange("p (k t) -> p k t", k=16)
        kvb = view(arena, 0, [128, 2, 8192], BF16)
        fa = view(arena, 32, [128, 1, CH], F32)
        fb = view(arena, 34, [128, 1, CH], F32)
        fo = view(arena, 36, [128, 1, CH], F32)
        ft = view(arena, 38, [128, 1, CH], F32)
        lnb = view(arena, 40, [128, 1, 1024], F32)
        rsb = view(arena, 44, [128, 1, 1024], F32)
        sqb = view(arena, 48, [128, 1, CH], BF16)
        pT = view(big, 64, [128, 4, 1024], BF16)
        units = [("A", h, h) for h in range(8)] + [("B", h, 8 + h // 4) for h in range(8)]
        loaded = {}
        nload = [0]

        def load_kv(src_):
            slot_ = nload[0] % 2
            nload[0] += 1
            T.dma("sp", kvb[:, slot_, 0:4096], kT_scr[src_], [], [("kvK", slot_)], "d_kvK%d" % slot_)
            T.dma("sp", kvb[:, slot_, 4096:8192], v_scr[src_].rearrange("p k d -> p (k d)"), [], [("kvV", slot_)], "d_kvV%d" % slot_)
            loaded[src_] = slot_

        srcs = []
        for u in units:
            if u[2] not in srcs:
                srcs.append(u[2])
        load_kv(srcs[0])
        pending = deque()
        blocks = [(ui, br, h, src_, qb) for ui, (br, h, src_) in enumerate(units) for qb in range(2)]
        NIT = len(blocks) * 32

        def blk(i):
            ui, br, h, src_, qb = blocks[i // 32]
            kt = i % 32
            hu = h if br == "A" else 8 + h
            slot = loaded[src_]
            Kt = kvb[:, slot, 0:4096]
            Vt = kvb[:, slot, 4096:8192].rearrange("p (k d) -> p k d", k=32)
            return br, hu, src_, qb, kt, slot, Kt, Vt

        def pslot(i, br):
            if br == "A":
                return pT[:, i % 4, :], ("pT", i % 4, 0), ("pT", i % 4, 1)
            k = i % 8
            return pT[:, k // 2, (k % 2) * 512:(k % 2 + 1) * 512], ("pT", k // 2, k % 2), None

        def emit_S(i):
            br, hu, src_, qb, kt, slot, Kt, Vt = blk(i)
            sp_i = i % 2
            pap, pk0, pk1 = pslot(i, br)
            qsl = slice(qb * CH, (qb + 1) * CH)
            ksl = slice(kt * 128, (kt + 1) * 128)
            kK = ("kvK", slot)
            qkey = ("qT", hu, qb)
            if br == "A":
                def smm(e):
                    e.matmul(psa(sp_i, 0), lhsT=Kt[0:64, ksl], rhs=qT[0:64, hu, qsl], start=True, stop=True)
                    return e.matmul(psa(sp_i, 1), lhsT=Kt[64:128, ksl], rhs=qT[64:128, hu, qsl], start=True, stop=True)
                T.op("pe", smm, [kK, qkey], [psk(sp_i, 0), psk(sp_i, 1)])
                T.op("act", lambda e: e.activation(out=pap, in_=PS[sp_i][:], func=AF.Exp, scale=SC_A),
                     [psk(sp_i, 0), psk(sp_i, 1)], [pk0, pk1])
            else:
                sb_ = ((i % 4) // 2, (i % 4) % 2)
                T.op("pe", lambda e: e.matmul(psa(*sb_), lhsT=Kt[:, ksl], rhs=qT[:, hu, qsl], start=True, stop=True),
                     [kK, qkey], [psk(*sb_)])
                T.op("act", lambda e: e.activation(out=pap, in_=psa(*sb_), func=AF.Exp, scale=SC_B),
                     [psk(*sb_)], [pk0])

        def emit_PV(i):
            br, hu, src_, qb, kt, slot, Kt, Vt = blk(i)
            pap, pk0, pk1 = pslot(i, br)
            kV = ("kvV", slot)
            if kt == 0 and qb == 0:
                si = srcs.index(src_)
                if si + 1 < len(srcs) and srcs[si + 1] not in loaded:
                    load_kv(srcs[si + 1])
            if br == "A":
                def pv(e):
                    e.matmul(psa(2, 0), lhsT=Vt[:, kt, :], rhs=pap[:, 0:512], start=(kt == 0), stop=(kt == 31))
                    e.matmul(psa(2, 1), lhsT=Vt[:, kt, :], rhs=pap[:, 512:1024], start=(kt == 0), stop=(kt == 31))
                    e.matmul(psa(3, 0), lhsT=ones, rhs=pap[:, 0:512], start=(kt == 0), stop=(kt == 31))
                    return e.matmul(psa(3, 1), lhsT=ones, rhs=pap[:, 512:1024], start=(kt == 0), stop=(kt == 31))
                T.op("pe", pv, [kV, pk0, pk1, "cm"], [psk(2, 0), psk(2, 1), psk(3, 0), psk(3, 1)])
            else:
                def pv(e):
                    e.matmul(psa(2, 0), lhsT=Vt[:, kt, :], rhs=pap, start=(kt == 0), stop=(kt == 31))
                    return e.matmul(psa(3, 0), lhsT=ones, rhs=pap, start=(kt == 0), stop=(kt == 31))
                T.op("pe", pv, [kV, pk0, "cm"], [psk(2, 0), psk(3, 0)])

        def emit_finish(i):
            br, hu, src_, qb, kt, slot, Kt, Vt = blk(i)
            qsl = slice(qb * CH, (qb + 1) * CH)
            fj = 0
            okey = ("ogT", hu, qb)
            zkey = ("zsT", hu, qb)
            while pending:
                pending.popleft()()
            if br == "A":
                T.op("act", lambda e: e.activation(out=lnb[:, fj, :], in_=PS[3][:], func=AF.Ln), [psk(3, 0), psk(3, 1)], [("lnb", fj)])
                T.op("act", lambda e: e.activation(out=rsb[:, fj, :], in_=lnb[:, fj, :], func=AF.Exp, scale=-1.0), [("lnb", fj)], [("rsb", fj)])
                T.op("dve", lambda e: e.tensor_tensor(out=fa[:, fj, :], in0=psa(2, 0), in1=rsb[:, fj, 0:512], op=ALU.mult), [psk(2, 0), ("rsb", fj)], [("fa", fj)])
                T.op("dve", lambda e: e.tensor_tensor(out=fb[:, fj, :], in0=psa(2, 1), in1=rsb[:, fj, 512:1024], op=ALU.mult), [psk(2, 1), ("rsb", fj)], [("fb", fj)])

                def s_o():
                    T.op("dve", lambda e: e.scalar_tensor_tensor(out=fo[:, fj, :], in0=fb[:, fj, :], scalar=neglam, in1=fa[:, fj, :], op0=ALU.mult, op1=ALU.add),
                         [("fa", fj), ("fb", fj)], [("fo", fj)])

                def s_sq():
                    T.op("act", lambda e: e.activation(out=sqb[:, fj, :], in_=fo[:, fj, :], func=AF.Square), [("fo", fj)], [("sqb", fj)])

                def s_mmln():
                    T.op("pe", lambda e: e.matmul(psa(qb, 0), lhsT=ones, rhs=sqb[:, fj, :], start=True, stop=True), [("sqb", fj), "cm"], [psk(qb, 0)])
                    T.op("act", lambda e: e.activation(out=lnb[:, fj, 0:CH], in_=psa(qb, 0), func=AF.Ln, scale=1.0 / 128, bias=epsb[:]), [psk(qb, 0)], [("lnb", fj)])

                def s_ex():
                    T.op("act", lambda e: e.activation(out=rsb[:, fj, 0:CH], in_=lnb[:, fj, 0:CH], func=AF.Exp, scale=-0.5), [("lnb", fj)], [("rsb", fj)])

                def s_g():
                    T.op("dve", lambda e: e.scalar_tensor_tensor(out=ft[:, fj, :], in0=fo[:, fj, :], scalar=gsub, in1=rsb[:, fj, 0:CH], op0=ALU.mult, op1=ALU.mult),
                         [("fo", fj), ("rsb", fj)], [("ft", fj)])

                def s_z():
                    T.op("pool", lambda e: e.tensor_tensor(out=ogT[:, hu, qsl], in0=ft[:, fj, :], in1=zsT[:, hu, qsl], op=ALU.mult), [("ft", fj), zkey], [okey])
                pending.extend([s_o, s_sq, s_mmln, s_ex, s_g, s_z])
            else:
                T.op("act", lambda e: e.activation(out=lnb[:, fj, 0:CH], in_=psa(3, 0), func=AF.Ln), [psk(3, 0)], [("lnb", fj)])
                T.op("act", lambda e: e.activation(out=rsb[:, fj, 0:CH], in_=lnb[:, fj, 0:CH], func=AF.Exp, scale=-1.0), [("lnb", fj)], [("rsb", fj)])
                T.op("dve", lambda e: e.tensor_tensor(out=ft[:, fj, :], in0=psa(2, 0), in1=rsb[:, fj, 0:CH], op=ALU.mult), [psk(2, 0), ("rsb", fj)], [("ft", fj)])

                def s_z():
                    T.op("pool", lambda e: e.tensor_tensor(out=ogT[:, hu, qsl], in0=ft[:, fj, :], in1=zsT[:, hu, qsl], op=ALU.mult), [("ft", fj), zkey], [okey])
                pending.extend([s_z])

        s_emitted = 0
        for i in range(NIT):
            look = 1 if blocks[i // 32][1] == "A" else 2
            while s_emitted < min(NIT, i + 1 + look):
                nsrc = blocks[s_emitted // 32][3]
                if nsrc not in loaded:
                    load_kv(nsrc)
                if blocks[s_emitted // 32][1] == "A" and s_emitted > i + 1:
                    break
                emit_S(s_emitted)
                s_emitted += 1
            emit_PV(i)
            if i % 32 == 31:
                emit_finish(i)
            elif pending and i % 32 >= 2:
                pending.popleft()()
        while pending:
            pending.popleft()()
        T.barrier()
        if debug == 3:
            dtmp = view(arena, 0, [128, NOWN], F32)
            for hu in range(16):
                T.op("dve", lambda e, hu=hu: e.tensor_copy(out=dtmp, in_=ogT[:, hu, :]), [], ["dtmp"])
                T.dma("sp", dbg["og"][:, hu, :], dtmp, ["dtmp"], [], "d_dbg")
            T.barrier()
            return nc

        mT = view(arena, 0, [128, 16, NOWN], BF16)
        gsl = [[view(big, (s * 2 + g) * 8, [128, KC, 256], BF16) for g in range(2)] for s in range(2)]
        wo_ab = view(big, 32, [128, 2, 2, 8, 256], BF16)
        lnb = view(big, 48, [128, 2, 1024], F32)
        rsb = view(big, 56, [128, 2, 1024], F32)

        def load_p4(db):
            s_ = db % 2
            load_wblock(gsl[s_][0], w_in, C_GA + db * 256, 256, ("gsl", s_, 0), "d_gsl%d0" % s_)
            load_wblock(gsl[s_][1], w_in, C_GB + db * 256, 256, ("gsl", s_, 1), "d_gsl%d1" % s_)
            T.dma("pool", wo_ab[:, s_, 0], w_oa[:, db * 256:(db + 1) * 256].rearrange("(f p) n -> p f n", p=128), [], [("woab", s_, 0)], "d_woa%d" % s_)
            T.dma("pool", wo_ab[:, s_, 1], w_ob[:, db * 256:(db + 1) * 256].rearrange("(f p) n -> p f n", p=128), [], [("woab", s_, 1)], "d_wob%d" % s_)

        load_p4(0)
        for db in range(8):
            s = db % 2
            for c2 in range(2):
                dc = db * 2 + c2
                csl = slice(c2 * 128, (c2 + 1) * 128)
                for tc in range(2):
                    if c2 == 0 and tc == 1 and db + 1 < 8:
                        load_p4(db + 1)
                    tsl = slice(tc * CH, (tc + 1) * CH)
                    hk = [("hT_own", 0), ("hT_own", 1)]
                    pg, py = 2 * tc, 2 * tc + 1
                    proj_fm((pg, 0), lambda kc: gsl[s][0][:, kc, csl], lambda kc: hT_own[:, kc, tsl], hk + [("gsl", s, 0)])
                    proj_fm((pg, 1), lambda kc: gsl[s][1][:, kc, csl], lambda kc: hT_own[:, kc, tsl], hk + [("gsl", s, 1)])

                    def ya(e, br_, pb):
                        last = None
                        for fc in range(8):
                            last = e.matmul(psa(*pb), lhsT=wo_ab[:, s, br_, fc, csl], rhs=ogT[:, br_ * 8 + fc, tsl], start=(fc == 0), stop=(fc == 7))
                        return last
                    ogk = [("ogT", hh, tc) for hh in range(16)]
                    T.op("pe", lambda e: ya(e, 0, (py, 0)), ogk + [("woab", s, 0)], [psk(py, 0)])
                    T.op("pe", lambda e: ya(e, 1, (py, 1)), ogk + [("woab", s, 1)], [psk(py, 1)])
                    j = tc
                    T.op("act", lambda e: e.activation(out=lnb[:, j, :], in_=PS[pg][:], func=AF.Sigmoid), [psk(pg, 0), psk(pg, 1)], [("lnb", j)])
                    T.op("dve", lambda e: e.tensor_tensor(out=rsb[:, j, :], in0=PS[py][:], in1=lnb[:, j, :], op=ALU.mult), [psk(py, 0), psk(py, 1), ("lnb", j)], [("rsb", j)])
                    T.op("pool", lambda e: e.tensor_tensor(out=mT[:, dc, tsl], in0=rsb[:, j, 0:512], in1=rsb[:, j, 512:1024], op=ALU.add), [("rsb", j)], [("mT", dc, tc)])
        T.barrier()
        if debug == 4:
            dtmp = view(big, 0, [128, NOWN], F32)
            for hu in range(16):
                T.op("dve", lambda e, hu=hu: e.tensor_copy(out=dtmp, in_=mT[:, hu, :]), [], ["dtmp"])
                T.dma("sp", dbg["mg"][:, hu, :], dtmp, ["dtmp"], [], "d_dbg")
            T.barrier()
            return nc

        wo_sb = big[:, 0:32768].rearrange("p (k n) -> p k n", k=KC)
        junk5 = view(big, 64, [128, D], BF16)
        xs5 = view(arena, 32, [128, 2, D], F32)
        gbc = view(arena, 48, [128, D], F32)
        yb = bufB[:, 0:8192].bitcast(F32).rearrange("p (s n) -> p s n", s=2)
        for cb in range(4):
            T.dma("pool", wo_sb[:, :, cb * 512:(cb + 1) * 512], w_o[:, cb * 512:(cb + 1) * 512].rearrange("(k p) n -> p k n", p=128), [], [("wo", cb)], "d_wo%d" % cb)
        T.dma("sp", gbc, final_g.partition_broadcast(128), [], ["gbc"], "d_gbc")
        for tt in range(8):
            xb = tt % 2
            xk = ("xs", xb)
            T.dma("sp", xs5[:, xb, :], xkv[tt * 128:(tt + 1) * 128, :], [], [xk], "d_xs%d" % xb)
            yk = ("yb", xb)
            for cb in range(4):
                pb = (cb // 2, cb % 2) if tt % 2 == 0 else (2 + cb // 2, cb % 2)

                def f(e):
                    last = None
                    for kc in range(KC):
                        last = e.matmul(psa(*pb), lhsT=mT[:, kc, tt * 128:(tt + 1) * 128], rhs=wo_sb[:, kc, cb * 512:(cb + 1) * 512], start=(kc == 0), stop=(kc == KC - 1))
                    return last
                T.op("pe", f, [("mT", kc, tt // 4) for kc in range(KC)] + [("wo", cb)], [psk(*pb)])
                T.op("dve", lambda e: e.tensor_tensor(out=yb[:, xb, cb * 512:(cb + 1) * 512], in0=psa(*pb), in1=xs5[:, xb, cb * 512:(cb + 1) * 512], op=ALU.add),
                     [psk(*pb), xk], [yk])
            rstd, sk = rms_stats(yb[:, xb, :], yk, xb, junk5, "junk5")
            T.op("dve", lambda e: e.scalar_tensor_tensor(out=xs5[:, xb, :], in0=yb[:, xb, :], scalar=rstd, in1=gbc, op0=ALU.mult, op1=ALU.mult), [yk, sk, "gbc"], [xk])
            T.dma("sp", out[tt * 128:(tt + 1) * 128, :], xs5[:, xb, :], [xk], [], "d_out%d" % xb)
        T.barrier()
    return nc


def _rope_tables():
    inv = (1.0 / (10000.0 ** (np.arange(0, 64, 2, dtype=np.float32) / np.float32(64)))).astype(np.float32)
    pos = np.arange(S, dtype=np.float32)
    angA = pos[:, None] * inv[None, :]
    angR = (np.arange(S) // 64).astype(np.float32)[:, None] * inv[None, :]
    angC = (np.arange(S) % 64).astype(np.float32)[:, None] * inv[None, :]
    p = np.arange(128)
    f = p % 32
    sign = np.where((p % 64) < 32, -1.0, 1.0).astype(np.float32)
    cosA = np.cos(angA)[:, f].T.astype(np.float32)
    sinA = (np.sin(angA)[:, f].T * sign[:, None]).astype(np.float32)
    angB = np.where((p < 64)[None, :], angR[:, f], angC[:, f])
    cosB = np.cos(angB).T.astype(np.float32)
    sinB = (np.sin(angB).T * sign[:, None]).astype(np.float32)
    return cosA, sinA, cosB, sinB


def _const_mats():
    ident = np.eye(128, dtype=np.float32)
    p = np.arange(128)
    partner = np.where((p % 64) < 32, p + 32, p - 32)
    perm = np.zeros((128, 128), np.float32)
    perm[partner, p] = 1.0
    ones = np.ones((128, 128), np.float32)
    return np.ascontiguousarray(np.stack([ident, perm, ones], axis=1))


def make_in_maps(x, norm_g, w_in, lambda_q1, lambda_k1, lambda_q2, lambda_k2, subln_g, q_norm_g, k_norm_g,
                 w_out_a, w_out_b, w_o, final_g):
    f32 = lambda a: np.ascontiguousarray(np.asarray(a, dtype=np.float32))
    x = f32(x)
    tabs = _rope_tables()
    cm = _const_mats()
    lam4 = f32(np.concatenate([np.asarray(lambda_q1)[0], np.asarray(lambda_k1)[0], np.asarray(lambda_q2)[0], np.asarray(lambda_k2)[0]]))
    gvecs = f32(np.stack([np.asarray(subln_g)[0], np.asarray(q_norm_g)[0], np.asarray(k_norm_g)[0]], axis=1))
    w_in0, woa0, wob0, wo0 = f32(np.asarray(w_in)[0]), f32(np.asarray(w_out_a)[0]), f32(np.asarray(w_out_b)[0]), f32(np.asarray(w_o)[0])
    ng, fg = f32(np.asarray(norm_g)[0]), f32(final_g)
    maps = []
    for core in range(8):
        b, c = core // 4, core % 4
        order = np.roll(np.arange(S), -c * NOWN)
        m = {"xkv": np.ascontiguousarray(x[b][order]), "w_in": w_in0, "w_out_a": woa0, "w_out_b": wob0, "w_o": wo0,
             "norm_g": ng, "final_g": fg, "lam4": lam4, "gvecs": gvecs, "cmats": cm}
        for n, t in zip(("cosA", "sinA", "cosB", "sinB"), tabs):
            m[n] = np.ascontiguousarray(t[:, order])
        maps.append(m)
    return maps


def kernel(**inputs):
    nc = build_nc()
    maps = make_in_maps(**inputs)
    res = run_bass_kernel_spmd(nc, maps, core_ids=list(range(8)))
    out = np.empty((2, S, D), np.float32)
    for core in range(8):
        b, c = core // 4, core % 4
        out[b, c * NOWN:(c + 1) * NOWN] = res.results[core]["out"]
    return out
```

```python
import math
from collections import deque
from contextlib import ExitStack

import numpy as np
import concourse.bass as bass
import concourse.mybir as mybir
from concourse.bass_utils import run_bass_kernel_spmd

F32 = mybir.dt.float32
BF16 = mybir.dt.bfloat16
AF = mybir.ActivationFunctionType
ALU = mybir.AluOpType

D = 2048
KC = 16
S = 4096
NOWN = 1024
CH = 512
NCH = S // CH
COLS = 10752
EPS = 1e-6
LAM_INIT = 0.8 - 0.6 * math.exp(-0.3 * 0)
C_QA, C_KA, C_VA, C_ZA, C_QB, C_KB, C_VB, C_ZB, C_GA, C_GB = 0, 1024, 2048, 3072, 4096, 5120, 5376, 5632, 6656, 8704
SC_A = 64 ** -0.5
SC_B = 128 ** -0.5


class Tracker:
    def __init__(self, nc, stack):
        self.nc = nc
        self.stack = stack
        self.eng = {"pe": nc.tensor, "act": nc.scalar, "dve": nc.vector, "pool": nc.gpsimd, "sp": nc.sync}
        self.sems = {}
        self.cnt = {}
        self.waited = {e: {} for e in self.eng}
        self.lastw = {}
        self.readers = {}
        for e in self.eng:
            self._sem("e_" + e)

    def _sem(self, name):
        if name not in self.sems:
            self.sems[name] = self.stack.enter_context(self.nc.semaphore(name))
            self.cnt[name] = 0
        return self.sems[name]

    def _wait_for(self, e, toks):
        need = {}
        for t in toks:
            if t is None:
                continue
            s, v = t
            if v > need.get(s, 0):
                need[s] = v
        for s, v in need.items():
            if self.waited[e].get(s, 0) >= v:
                continue
            if e == "pe" and s == "e_pe":
                continue
            self.eng[e].wait_ge(self.sems[s], v)
            self.waited[e][s] = v

    def _deps(self, reads, writes, e=None):
        toks = []
        for k in reads:
            toks.append(self.lastw.get(k))
        own = "e_" + e if e else None
        for k in writes:
            for t in [self.lastw.get(k)] + list(self.readers.get(k, ())):
                if t is not None and t[0] != own:
                    toks.append(t)
        return toks

    def _record(self, tok, reads, writes):
        for k in reads:
            self.readers.setdefault(k, []).append(tok)
        for k in writes:
            self.lastw[k] = tok
            self.readers[k] = []

    @staticmethod
    def _excl(reads, writes):
        ps_reads = [k for k in reads if isinstance(k, tuple) and k and k[0] == "ps"]
        if not ps_reads:
            return reads, writes
        return [k for k in reads if k not in ps_reads], list(writes) + ps_reads

    def op(self, e, fn, reads=(), writes=()):
        reads, writes = self._excl(reads, writes)
        self._wait_for(e, self._deps(reads, writes, e))
        ins = fn(self.eng[e])
        s = "e_" + e
        self.cnt[s] += 1
        ins.then_inc(self.sems[s], 1)
        self._record((s, self.cnt[s]), reads, writes)

    def dma(self, q, out, in_, reads, writes, sem):
        self._sem(sem)
        self._wait_for(q, self._deps(reads, writes))
        ins = self.eng[q].dma_start(out=out, in_=in_)
        self.cnt[sem] += 16
        ins.then_inc(self.sems[sem], 16)
        self._record((sem, self.cnt[sem]), reads, writes)

    def barrier(self):
        toks = [(s, c) for s, c in self.cnt.items() if c > 0]
        for e in self.eng:
            self._wait_for(e, toks)
        self.lastw.clear()
        self.readers.clear()


def build_nc(debug=None):
    nc = bass.Bass("TRN2", target_bir_lowering=False)
    dt = nc.dram_tensor
    xkv = dt("xkv", [S, D], F32, kind="ExternalInput").ap()
    w_in = dt("w_in", [D, COLS], F32, kind="ExternalInput").ap()
    w_oa = dt("w_out_a", [1024, D], F32, kind="ExternalInput").ap()
    w_ob = dt("w_out_b", [1024, D], F32, kind="ExternalInput").ap()
    w_o = dt("w_o", [D, D], F32, kind="ExternalInput").ap()
    norm_g = dt("norm_g", [D], F32, kind="ExternalInput").ap()
    final_g = dt("final_g", [D], F32, kind="ExternalInput").ap()
    lam4 = dt("lam4", [256], F32, kind="ExternalInput").ap()
    gvecs = dt("gvecs", [128, 3], F32, kind="ExternalInput").ap()
    tabs = {n: dt(n, [128, S], F32, kind="ExternalInput").ap() for n in ("cosA", "sinA", "cosB", "sinB")}
    cmats = dt("cmats", [128, 3, 128], F32, kind="ExternalInput").ap()
    out = dt("out", [NOWN, D], F32, kind="ExternalOutput").ap()
    kT_scr = dt("kT_scr", [10, 128, S], BF16, kind="Internal").ap()
    v_scr = dt("v_scr", [10, 128, 32, 128], BF16, kind="Internal").ap()
    dbg = {}
    if debug:
        dbg["qT"] = dt("dbg_qT", [128, 16, NOWN], F32, kind="ExternalOutput").ap()
        dbg["zs"] = dt("dbg_zs", [128, 16, NOWN], F32, kind="ExternalOutput").ap()
        dbg["kT"] = dt("dbg_kT", [10, 128, S], BF16, kind="ExternalOutput").ap()
        dbg["v"] = dt("dbg_v", [10, 128, 32, 128], BF16, kind="ExternalOutput").ap()
        dbg["og"] = dt("dbg_og", [128, 16, NOWN], F32, kind="ExternalOutput").ap()
        dbg["mg"] = dt("dbg_mg", [128, 16, NOWN], F32, kind="ExternalOutput").ap()

    with ExitStack() as st:
        E = st.enter_context
        T = Tracker(nc, st)
        sb = lambda name, shape, dtp: E(nc.sbuf_tensor(name, shape, dtp))

        big = sb("big", [128, 40960], BF16)
        hT_own = sb("hT_own", [128, KC, NOWN], BF16)
        bufB = sb("bufB", [128, 16384], BF16)
        arena = sb("arena", [128, 29184], BF16)
        cm = sb("cm", [128, 3, 128], BF16)
        gv = sb("gv", [128, 3], F32)
        g16 = sb("g16", [128, KC], F32)
        lamb = sb("lamb", [128, 256], F32)
        lamt = sb("lamt", [128, 8], F32)
        epsb = sb("epsb", [128, 1], F32)
        st8 = sb("st8", [128, 16], F32)

        def view(buf, off_kib, shape, dtp):
            esz = 4 if dtp == F32 else 2
            n = 1
            for s_ in shape[1:]:
                n *= s_
            a_ = int(off_kib * 1024) // 2
            ap = buf[:, a_:a_ + n * esz // 2]
            if dtp == F32:
                ap = ap.bitcast(F32)
            if len(shape) == 3:
                ap = ap.rearrange("p (a b) -> p a b", a=shape[1])
            elif len(shape) == 4:
                ap = ap.rearrange("p (a b c) -> p a b c", a=shape[1], b=shape[2])
            elif len(shape) == 5:
                ap = ap.rearrange("p (a b c d) -> p a b c d", a=shape[1], b=shape[2], c=shape[3])
            return ap

        xs = view(arena, 0, [128, 2, D], F32)
        hb2 = view(arena, 16, [128, 2, D], BF16)
        tab = view(arena, 24, [128, 4, CH], F32)
        kb = view(arena, 32, [128, 2, CH], BF16)
        t1 = view(arena, 34, [128, 2, CH], F32)
        t2 = view(arena, 38, [128, 2, CH], F32)
        lnb = view(arena, 42, [128, 2, CH], F32)
        rsb = view(arena, 46, [128, 2, CH], F32)
        sqb = view(arena, 50, [128, 2, CH], BF16)
        kout = view(arena, 52, [128, 2, CH], BF16)
        vout = view(arena, 54, [128, 1280], BF16)
        PS = [E(nc.psum_tensor("ps%d" % i, [128, 1024], F32)) for i in range(4)]

        ident = cm[:, 0, :]
        perm = cm[:, 1, :]
        ones = cm[:, 2, :]

        def psk(i, h):
            return ("ps", i, h)

        def psa(i, h):
            return PS[i][:, h * 512:(h + 1) * 512]

        T.dma("pool", cm[:], cmats, [], ["cm"], "d_cm")
        T.dma("sp", gv[:], gvecs, [], ["gv"], "d_gv")
        T.dma("sp", lamb[:], lam4.partition_broadcast(128), [], ["lamb"], "d_lam")
        with nc.allow_non_contiguous_dma(reason="tiny gain vector transpose load"):
            T.dma("sp", g16[:], norm_g.rearrange("(k p) -> p k", p=128), [], ["g16"], "d_g16")
        T.op("dve", lambda e: e.memset(epsb[:], EPS), [], ["epsb"])
        T.op("dve", lambda e: e.tensor_tensor(out=t1[:, 0, 0:64], in0=lamb[:, 0:64], in1=lamb[:, 64:128], op=ALU.mult), ["lamb"], ["t1c"])
        T.op("dve", lambda e: e.tensor_tensor(out=t1[:, 0, 64:128], in0=lamb[:, 128:192], in1=lamb[:, 192:256], op=ALU.mult), ["lamb", "t1c"], ["t1c"])
        T.op("dve", lambda e: e.reduce_sum(out=lamt[:, 0:1], in_=t1[:, 0, 0:64], axis=mybir.AxisListType.X), ["t1c"], ["lamt"])
        T.op("dve", lambda e: e.reduce_sum(out=lamt[:, 1:2], in_=t1[:, 0, 64:128], axis=mybir.AxisListType.X), ["t1c", "lamt"], ["lamt"])
        T.op("act", lambda e: e.activation(out=lamt[:, 2:4], in_=lamt[:, 0:2], func=AF.Exp), ["lamt"], ["lamt"])
        T.op("dve", lambda e: e.scalar_tensor_tensor(out=lamt[:, 4:5], in0=lamt[:, 3:4], scalar=-LAM_INIT, in1=lamt[:, 2:3], op0=ALU.add, op1=ALU.subtract), ["lamt"], ["lamt"])
        T.op("dve", lambda e: e.tensor_scalar(out=lamt[:, 5:6], in0=gv[:, 0:1], scalar1=(1.0 - LAM_INIT), scalar2=None, op0=ALU.mult), ["gv", "lamt"], ["lamt"])
        neglam = lamt[:, 4:5]
        gsub = lamt[:, 5:6]
        T.barrier()
        if debug == 0.1:
            return nc

        def load_wblock(dst3, src_w, col0, ncols, key, sem):
            T.dma("pool", dst3, src_w[:, col0:col0 + ncols].rearrange("(kc p) n -> p kc n", p=128), [], [key], sem)

        def rope_finish(src_ap, src_keys, j, cos_ap, sin_ap, tab_keys, out_ap, out_key, rot_ps, inplace=False):
            ri, rh = rot_ps
            T.op("act", lambda e: e.activation(out=kb[:, j, :], in_=src_ap, func=AF.Copy), src_keys, [("kb", j)])
            T.op("pe", lambda e: e.matmul(psa(ri, rh), lhsT=perm, rhs=kb[:, j, :], start=True, stop=True), [("kb", j), "cm"], [psk(ri, rh)])
            T.op("dve", lambda e: e.tensor_tensor(out=t1[:, j, :], in0=src_ap, in1=cos_ap, op=ALU.mult), list(src_keys) + list(tab_keys), [("t1", j)])
            T.op("dve", lambda e: e.tensor_tensor(out=t2[:, j, :], in0=psa(ri, rh), in1=sin_ap, op=ALU.mult), [psk(ri, rh)] + list(tab_keys), [("t2", j)])
            T.op("pool", lambda e: e.tensor_tensor(out=out_ap, in0=t1[:, j, :], in1=t2[:, j, :], op=ALU.add), [("t1", j), ("t2", j)], [out_key])

        def headnorm(src_ps, src_key, j, gcol, aux_ps):
            ai, ah = aux_ps
            T.op("act", lambda e: e.activation(out=sqb[:, j, :], in_=src_ps, func=AF.Square), [src_key], [("sqb", j)])
            T.op("pe", lambda e: e.matmul(psa(ai, ah), lhsT=ones, rhs=sqb[:, j, :], start=True, stop=True), [("sqb", j), "cm"], [psk(ai, ah)])
            T.op("act", lambda e: e.activation(out=lnb[:, j, 0:CH], in_=psa(ai, ah), func=AF.Ln, scale=1.0 / 128, bias=epsb[:]), [psk(ai, ah), "epsb"], [("lnb", j)])
            T.op("act", lambda e: e.activation(out=rsb[:, j, 0:CH], in_=lnb[:, j, 0:CH], func=AF.Exp, scale=-0.5), [("lnb", j)], [("rsb", j)])
            T.op("dve", lambda e: e.scalar_tensor_tensor(out=t1[:, j, :], in0=src_ps, scalar=gv[:, gcol:gcol + 1], in1=rsb[:, j, 0:CH], op0=ALU.mult, op1=ALU.mult), [src_key, ("rsb", j), "gv"], [("t1", j)])

        def proj_fm(ps_idx, wfun, rfun, reads):
            pi, ph = ps_idx

            def f(e):
                last = None
                for kc in range(KC):
                    last = e.matmul(psa(pi, ph), lhsT=wfun(kc), rhs=rfun(kc), start=(kc == 0), stop=(kc == KC - 1))
                return last
            T.op("pe", f, reads, [psk(pi, ph)])

        def rms_stats(src_ap, src_key, slot, junk_ap, junk_key):
            sc = st8[:, 4 * slot:4 * slot + 4]
            sk = ("st8", slot)
            T.op("act", lambda e: e.activation(out=junk_ap, in_=src_ap, func=AF.Square, accum_out=sc[:, 0:1]), [src_key], [junk_key, sk])
            T.op("act", lambda e: e.activation(out=sc[:, 1:2], in_=sc[:, 0:1], func=AF.Ln, scale=1.0 / D, bias=epsb[:]), [sk, "epsb"], [sk])
            T.op("act", lambda e: e.activation(out=sc[:, 2:3], in_=sc[:, 1:2], func=AF.Exp, scale=-0.5), [sk], [sk])
            return sc[:, 2:3], sk

        dq = []
        tickc = [0]

        def defer(n, fn):
            dq.append([tickc[0] + n, fn])

        def tick():
            tickc[0] += 1
            for d_ in [d_ for d_ in dq if d_[0] <= tickc[0]]:
                dq.remove(d_)
                d_[1]()

        def flush():
            while dq:
                dq.pop(0)[1]()

        wkv = big[:, 0:KC * 2560].rearrange("p (k n) -> p k n", k=KC)
        wkv_blocks = [(0, C_KA, 512), (512, C_KA + 512, 512), (1024, C_KB, 256),
                      (1280, C_VA, 512), (1792, C_VA + 512, 512), (2304, C_VB, 256)]
        for bi, (dcol, scol, n) in enumerate(wkv_blocks):
            load_wblock(wkv[:, :, dcol:dcol + n], w_in, scol, n, ("wkv", bi), "d_wkv%d" % bi)
        wkv_keys = [("wkv", bi) for bi in range(6)]
        if debug == 0.2:
            T.barrier()
            return nc
        hT_rot = bufB[:].rearrange("p (b k t) -> p b k t", b=2, k=KC)
        kcount = 0

        def tile_dst(ch, t4):
            if ch < 2:
                return hT_own, ("hT_own", ch), ch * CH + t4 * 128
            return hT_rot[:, ch % 2], ("hT_rot", ch % 2), t4 * 128

        def prepX(tt):
            if tt >= 32:
                return
            xbuf = tt % 2
            T.dma("sp", xs[:, xbuf, :], xkv[tt * 128:(tt + 1) * 128, :], [], [("xs", xbuf)], "d_xs%d" % xbuf)

        def prepA(ch, t4):
            tt = ch * 4 + t4
            xbuf = tt % 2
            xk = ("xs", xbuf)
            rstd, sk = rms_stats(xs[:, xbuf, :], xk, xbuf, hb2[:, xbuf, :], ("hb", xbuf))
            T.op("act", lambda e: e.activation(out=hb2[:, xbuf, :], in_=xs[:, xbuf, :], func=AF.Copy, scale=rstd), [xk, sk], [("hb", xbuf)])
            prepX(tt + 2)

        def prepB(ch, t4):
            tt = ch * 4 + t4
            xbuf = tt % 2
            dstT, dkey, tok_off = tile_dst(ch, t4)
            pst = PS[3][:].bitcast(BF16)

            def tr(e):
                last = None
                for kc in range(KC):
                    last = e.transpose(pst[:, kc * 128:(kc + 1) * 128], hb2[:, xbuf, kc * 128:(kc + 1) * 128], ident)
                return last
            T.op("pe", tr, [("hb", xbuf), "cm"], [psk(3, 0), psk(3, 1)])
            for kc in range(KC):
                T.op("dve", lambda e, kc=kc: e.tensor_scalar(out=dstT[:, kc, tok_off:tok_off + 128], in0=pst[:, kc * 128:(kc + 1) * 128],
                                                             scalar1=g16[:, kc:kc + 1], scalar2=None, op0=ALU.mult),
                     [psk(3, 0), psk(3, 1), "g16"], [dkey])

        def load_tabs(ch, which):
            for ti in which:
                tn = ("cosA", "sinA", "cosB", "sinB")[ti]
                T.dma("sp", tab[:, ti, :], tabs[tn][:, ch * CH:(ch + 1) * CH], [], [("tab", ti)], "d_tab%d" % ti)

        load_tabs(0, (0, 1, 2, 3))
        prepX(0)
        prepX(1)
        if debug == 0.31:
            prepA(0, 0)
            T.barrier()
            return nc
        for t4 in range(4):
            prepA(0, t4)
            prepB(0, t4)
        if debug == 0.3:
            T.barrier()
            return nc
        for ch in range(NCH):
            if ch < 2:
                hT = hT_own[:, :, ch * CH:(ch + 1) * CH]
                hkey = ("hT_own", ch)
            else:
                hT = hT_rot[:, ch % 2, :, :]
                hkey = ("hT_rot", ch % 2)
            for cc in range(10):
                pb = (cc % 2, 0)
                proj_fm(pb, lambda kc, cc=cc: wkv[:, kc, cc * 128:(cc + 1) * 128], lambda kc: hT[:, kc, :], [hkey] + wkv_keys)
                tick()
                if ch + 1 < NCH and cc % 2 == 0:
                    if cc // 2 < 4:
                        prepA(ch + 1, cc // 2)
                    if 1 <= cc // 2 <= 4:
                        prepB(ch + 1, cc // 2 - 1)
                j = cc % 2
                ko = kcount % 2
                kcount += 1

                def store(cc=cc, ko=ko, ch=ch):
                    T.dma("pool", kT_scr[cc, :, ch * CH:(ch + 1) * CH], kout[:, ko, :], [("kout", ko)], [], "d_kout%d" % ko)

                if cc < 8:
                    def postA(cc=cc, pb=pb, j=j, ko=ko, ch=ch, store=store):
                        rope_finish(psa(*pb), [psk(*pb)], j, tab[:, 0, :], tab[:, 1, :], [("tab", 0), ("tab", 1)], kout[:, ko, :], ("kout", ko), (cc % 2, 1))
                        store()
                        if cc == 7 and ch + 1 < NCH:
                            load_tabs(ch + 1, (0, 1))
                    defer(1, postA)
                else:
                    def postB1(cc=cc, pb=pb, j=j):
                        headnorm(psa(*pb), psk(*pb), j, 2, (cc % 2, 1))

                    def postB2(cc=cc, pb=pb, j=j, ko=ko, ch=ch, store=store):
                        rope_finish(t1[:, j, :], [("t1", j)], j, tab[:, 2, :], tab[:, 3, :], [("tab", 2), ("tab", 3)], kout[:, ko, :], ("kout", ko), (cc % 2, 1))
                        store()
                        if cc == 9 and ch + 1 < NCH:
                            load_tabs(ch + 1, (2, 3))
                    defer(1, postB1)
                    defer(2, postB2)
            for t4 in range(4):
                tt = ch * 4 + t4
                vkey = "vout"
                for blk, (c0, n) in enumerate(((1280, 512), (1792, 512), (2304, 256))):
                    pb = (2, blk % 2)

                    def f(e, c0=c0, n=n, pb=pb, t4=t4):
                        last = None
                        for kc in range(KC):
                            last = e.matmul(psa(*pb)[:, 0:n], lhsT=hT[:, kc, t4 * 128:(t4 + 1) * 128], rhs=wkv[:, kc, c0:c0 + n], start=(kc == 0), stop=(kc == KC - 1))
                        return last
                    T.op("pe", f, [hkey] + wkv_keys, [psk(*pb)])
                    tick()
                    T.op("act", lambda e, c0=c0, n=n, pb=pb: e.activation(out=vout[:, c0 - 1280:c0 - 1280 + n], in_=psa(*pb)[:, 0:n], func=AF.Copy), [psk(*pb)], [vkey])
                T.dma("act", v_scr[:, :, tt, :].rearrange("h p d -> p h d"), vout[:].rearrange("p (h d) -> p h d", h=10), [vkey], [], "d_vout")
        flush()
        T.barrier()

        if debug == 1:
            T.dma("sp", dbg["kT"], kT_scr, [], [], "d_dbg")
            T.dma("sp", dbg["v"], v_scr, [], [], "d_dbg")
            T.barrier()
            return nc

        qT = big[:, 0:16384].rearrange("p (k t) -> p k t", k=16)
        zsT = big[:, 16384:32768].rearrange("p (k t) -> p k t", k=16)
        tab2 = view(big, 64, [128, 4, NOWN], F32)
        wst3 = [bufB[:, s * 8192:(s + 1) * 8192].rearrange("p (k n) -> p k n", k=KC) for s in range(2)]
        q_blocks = [("qA", C_QA), ("qA", C_QA + 512), ("zA", C_ZA), ("zA", C_ZA + 512),
                    ("qB", C_QB), ("qB", C_QB + 512), ("zB", C_ZB), ("zB", C_ZB + 512)]
        if debug in (1.5, 1.6, 1.7):
            q_blocks = {1.5: q_blocks[0:1], 1.6: q_blocks[2:3], 1.7: q_blocks[4:5]}[debug]
        load_wblock(wst3[0], w_in, q_blocks[0][1], 512, ("wst", 0), "d_wst0")
        for ti, tn in enumerate(("cosA", "sinA", "cosB", "sinB")):
            T.dma("sp", tab2[:, ti, :], tabs[tn][:, 0:NOWN], [], [("tab2", ti)], "d_tab%d" % ti)
        cnt = 0
        for bi, (kind, col0) in enumerate(q_blocks):
            slot = bi % 2
            for c4 in range(4):
                hu_local = (bi % 2) * 4 + c4
                for tc in range(2):
                    if c4 == 1 and tc == 0 and bi + 1 < len(q_blocks):
                        load_wblock(wst3[(bi + 1) % 2], w_in, q_blocks[bi + 1][1], 512, ("wst", (bi + 1) % 2), "d_wst%d" % ((bi + 1) % 2))
                    pb = (cnt % 2, 0)
                    aux = (cnt % 2, 1)
                    j = cnt % 2
                    cnt += 1
                    proj_fm(pb, lambda kc, c4=c4, slot=slot: wst3[slot][:, kc, c4 * 128:(c4 + 1) * 128],
                            lambda kc, tc=tc: hT_own[:, kc, tc * CH:(tc + 1) * CH], [("hT_own", 0), ("hT_own", 1), ("wst", slot)])
                    tsl = slice(tc * CH, (tc + 1) * CH)
                    tick()
                    if kind == "qA":
                        def postA(pb=pb, j=j, tsl=tsl, hu_local=hu_local, tc=tc, aux=aux):
                            rope_finish(psa(*pb), [psk(*pb)], j, tab2[:, 0, tsl], tab2[:, 1, tsl], [("tab2", 0), ("tab2", 1)], qT[:, hu_local, tsl], ("qT", hu_local, tc), aux)
                        defer(1, postA)
                    elif kind == "qB":
                        def postB1(pb=pb, j=j, aux=aux):
                            headnorm(psa(*pb), psk(*pb), j, 1, aux)

                        def postB2(pb=pb, j=j, tsl=tsl, hu_local=hu_local, tc=tc, aux=aux):
                            rope_finish(t1[:, j, :], [("t1", j)], j, tab2[:, 2, tsl], tab2[:, 3, tsl], [("tab2", 2), ("tab2", 3)], qT[:, 8 + hu_local, tsl], ("qT", 8 + hu_local, tc), aux)
                        defer(1, postB1)
                        defer(2, postB2)
                    else:
                        hu = hu_local + (0 if kind == "zA" else 8)
                        T.op("act", lambda e, hu=hu, tsl=tsl, pb=pb: e.activation(out=zsT[:, hu, tsl], in_=psa(*pb), func=AF.Silu), [psk(*pb)], [("zsT", hu, tc)])
        flush()
        T.barrier()
        if debug in (1.5, 1.6, 1.7):
            return nc
        if debug == 2:
            dtmp = view(arena, 0, [128, NOWN], F32)
            for hu in range(16):
                T.op("dve", lambda e, hu=hu: e.tensor_copy(out=dtmp, in_=qT[:, hu, :]), [], ["dtmp"])
                T.dma("sp", dbg["qT"][:, hu, :], dtmp, ["dtmp"], [], "d_dbg")
                T.op("dve", lambda e, hu=hu: e.tensor_copy(out=dtmp, in_=zsT[:, hu, :]), [], ["dtmp"])
                T.dma("sp", dbg["zs"][:, hu, :], dtmp, ["dtmp"], [], "d_dbg")
            T.barrier()
            return nc

        ogT = bufB[:].rearrange("p (k t) -> p k t", k=16)
        kvb = view(arena, 0, [128, 2, 8192], BF16)
        fa = view(arena, 32, [128, 1, CH], F32)
        fb = view(arena, 34, [128, 1, CH], F32)
        fo = view(arena, 36, [128, 1, CH], F32)
        ft = view(arena, 38, [128, 1, CH], F32)
        lnb = view(arena, 40, [128, 1, 1024], F32)
        rsb = view(arena, 44, [128, 1, 1024], F32)
        sqb = view(arena, 48, [128, 1, CH], BF16)
        pT = view(big, 64, [128, 4, 1024], BF16)
        tsum = view(arena, 50, [128, 2, 1024], BF16)
        units = [("A", h, h) for h in range(8)] + [("B", h, 8 + h // 4) for h in range(8)]
        loaded = {}
        nload = [0]

        def load_kv(src_):
            slot_ = nload[0] % 2
            nload[0] += 1
            T.dma("sp", kvb[:, slot_, 0:4096], kT_scr[src_], [], [("kvK", slot_)], "d_kvK%d" % slot_)
            T.dma("sp", kvb[:, slot_, 4096:8192], v_scr[src_].rearrange("p k d -> p (k d)"), [], [("kvV", slot_)], "d_kvV%d" % slot_)
            loaded[src_] = slot_

        srcs = []
        for u in units:
            if u[2] not in srcs:
                srcs.append(u[2])
        load_kv(srcs[0])
        pending = deque()
        zdefer = []
        blocks = [(ui, br, h, src_, qb) for ui, (br, h, src_) in enumerate(units) for qb in range(2)]
        NIT = len(blocks) * 32

        def blk(i):
            ui, br, h, src_, qb = blocks[i // 32]
            kt = i % 32
            hu = h if br == "A" else 8 + h
            slot = loaded[src_]
            Kt = kvb[:, slot, 0:4096]
            Vt = kvb[:, slot, 4096:8192].rearrange("p (k d) -> p k d", k=32)
            return br, hu, src_, qb, kt, slot, Kt, Vt

        def pslot(i, br):
            if br == "A":
                return pT[:, i % 4, :], ("pT", i % 4, 0), ("pT", i % 4, 1)
            k = i % 8
            return pT[:, k // 2, (k % 2) * 512:(k % 2 + 1) * 512], ("pT", k // 2, k % 2), None

        def emit_S(i):
            br, hu, src_, qb, kt, slot, Kt, Vt = blk(i)
            sp_i = i % 2
            pap, pk0, pk1 = pslot(i, br)
            qsl = slice(qb * CH, (qb + 1) * CH)
            ksl = slice(kt * 128, (kt + 1) * 128)
            kK = ("kvK", slot)
            qkey = ("qT", hu, qb)
            if br == "A":
                def smm(e):
                    e.matmul(psa(sp_i, 0), lhsT=Kt[0:64, ksl], rhs=qT[0:64, hu, qsl], start=True, stop=True)
                    return e.matmul(psa(sp_i, 1), lhsT=Kt[64:128, ksl], rhs=qT[64:128, hu, qsl], start=True, stop=True)
                T.op("pe", smm, [kK, qkey], [psk(sp_i, 0), psk(sp_i, 1)])
                T.op("act", lambda e: e.activation(out=pap, in_=PS[sp_i][:], func=AF.Exp, scale=SC_A),
                     [psk(sp_i, 0), psk(sp_i, 1)], [pk0, pk1])
            else:
                sb_ = ((i % 4) // 2, (i % 4) % 2)
                T.op("pe", lambda e: e.matmul(psa(*sb_), lhsT=Kt[:, ksl], rhs=qT[:, hu, qsl], start=True, stop=True),
                     [kK, qkey], [psk(*sb_)])
                T.op("act", lambda e: e.activation(out=pap, in_=psa(*sb_), func=AF.Exp, scale=SC_B),
                     [psk(*sb_)], [pk0])

        def emit_PV(i):
            br, hu, src_, qb, kt, slot, Kt, Vt = blk(i)
            pap, pk0, pk1 = pslot(i, br)
            kV = ("kvV", slot)
            if kt == 0 and qb == 0:
                si = srcs.index(src_)
                if si + 1 < len(srcs) and srcs[si + 1] not in loaded:
                    load_kv(srcs[si + 1])
            if br == "A":
                def pv(e):
                    e.matmul(psa(2, 0), lhsT=Vt[:, kt, :], rhs=pap[:, 0:512], start=(kt == 0), stop=(kt == 31))
                    return e.matmul(psa(2, 1), lhsT=Vt[:, kt, :], rhs=pap[:, 512:1024], start=(kt == 0), stop=(kt == 31))
                T.op("pe", pv, [kV, pk0, pk1], [psk(2, 0), psk(2, 1)])
            else:
                T.op("pe", lambda e: e.matmul(psa(2, 0), lhsT=Vt[:, kt, :], rhs=pap, start=(kt == 0), stop=(kt == 31)), [kV, pk0], [psk(2, 0)])
            if kt % 2 == 1:
                ts_ = (i // 2) % 2
                pap0, qk0, qk1 = pslot(i - 1, br)
                if br == "A":
                    T.op("dve", lambda e: e.tensor_tensor(out=tsum[:, ts_, :], in0=pap0, in1=pap, op=ALU.add), [qk0, qk1, pk0, pk1], [("tsum", ts_)])

                    def zmm():
                        def f(e):
                            e.matmul(psa(3, 0), lhsT=ones, rhs=tsum[:, ts_, 0:512], start=(kt == 1), stop=(kt == 31))
                            return e.matmul(psa(3, 1), lhsT=ones, rhs=tsum[:, ts_, 512:1024], start=(kt == 1), stop=(kt == 31))
                        T.op("pe", f, [("tsum", ts_), "cm"], [psk(3, 0), psk(3, 1)])
                else:
                    T.op("dve", lambda e: e.tensor_tensor(out=tsum[:, ts_, 0:512], in0=pap0, in1=pap, op=ALU.add), [qk0, pk0], [("tsum", ts_)])

                    def zmm():
                        T.op("pe", lambda e: e.matmul(psa(3, 0), lhsT=ones, rhs=tsum[:, ts_, 0:512], start=(kt == 1), stop=(kt == 31)), [("tsum", ts_), "cm"], [psk(3, 0)])
                if kt == 31:
                    zmm()
                else:
                    zdefer.append(zmm)
            elif zdefer:
                zdefer.pop(0)()

        def emit_finish(i):
            br, hu, src_, qb, kt, slot, Kt, Vt = blk(i)
            qsl = slice(qb * CH, (qb + 1) * CH)
            fj = 0
            okey = ("ogT", hu, qb)
            zkey = ("zsT", hu, qb)
            while pending:
                pending.popleft()()
            if br == "A":
                T.op("act", lambda e: e.activation(out=lnb[:, fj, :], in_=PS[3][:], func=AF.Ln), [psk(3, 0), psk(3, 1)], [("lnb", fj)])
                T.op("act", lambda e: e.activation(out=rsb[:, fj, :], in_=lnb[:, fj, :], func=AF.Exp, scale=-1.0), [("lnb", fj)], [("rsb", fj)])
                T.op("dve", lambda e: e.tensor_tensor(out=fa[:, fj, :], in0=psa(2, 0), in1=rsb[:, fj, 0:512], op=ALU.mult), [psk(2, 0), ("rsb", fj)], [("fa", fj)])
                T.op("dve", lambda e: e.tensor_tensor(out=fb[:, fj, :], in0=psa(2, 1), in1=rsb[:, fj, 512:1024], op=ALU.mult), [psk(2, 1), ("rsb", fj)], [("fb", fj)])

                def s_o():
                    T.op("dve", lambda e: e.scalar_tensor_tensor(out=fo[:, fj, :], in0=fb[:, fj, :], scalar=neglam, in1=fa[:, fj, :], op0=ALU.mult, op1=ALU.add),
                         [("fa", fj), ("fb", fj)], [("fo", fj)])

                def s_sq():
                    T.op("act", lambda e: e.activation(out=sqb[:, fj, :], in_=fo[:, fj, :], func=AF.Square), [("fo", fj)], [("sqb", fj)])

                def s_mmln():
                    T.op("pe", lambda e: e.matmul(psa(qb, 0), lhsT=ones, rhs=sqb[:, fj, :], start=True, stop=True), [("sqb", fj), "cm"], [psk(qb, 0)])
                    T.op("act", lambda e: e.activation(out=lnb[:, fj, 0:CH], in_=psa(qb, 0), func=AF.Ln, scale=1.0 / 128, bias=epsb[:]), [psk(qb, 0)], [("lnb", fj)])

                def s_ex():
                    T.op("act", lambda e: e.activation(out=rsb[:, fj, 0:CH], in_=lnb[:, fj, 0:CH], func=AF.Exp, scale=-0.5), [("lnb", fj)], [("rsb", fj)])

                def s_g():
                    T.op("dve", lambda e: e.scalar_tensor_tensor(out=ft[:, fj, :], in0=fo[:, fj, :], scalar=gsub, in1=rsb[:, fj, 0:CH], op0=ALU.mult, op1=ALU.mult),
                         [("fo", fj), ("rsb", fj)], [("ft", fj)])

                def s_z():
                    T.op("pool", lambda e: e.tensor_tensor(out=ogT[:, hu, qsl], in0=ft[:, fj, :], in1=zsT[:, hu, qsl], op=ALU.mult), [("ft", fj), zkey], [okey])
                pending.extend([s_o, s_sq, s_mmln, s_ex, s_g, s_z])
            else:
                T.op("act", lambda e: e.activation(out=lnb[:, fj, 0:CH], in_=psa(3, 0), func=AF.Ln), [psk(3, 0)], [("lnb", fj)])
                T.op("act", lambda e: e.activation(out=rsb[:, fj, 0:CH], in_=lnb[:, fj, 0:CH], func=AF.Exp, scale=-1.0), [("lnb", fj)], [("rsb", fj)])
                T.op("dve", lambda e: e.tensor_tensor(out=ft[:, fj, :], in0=psa(2, 0), in1=rsb[:, fj, 0:CH], op=ALU.mult), [psk(2, 0), ("rsb", fj)], [("ft", fj)])

                def s_z():
                    T.op("pool", lambda e: e.tensor_tensor(out=ogT[:, hu, qsl], in0=ft[:, fj, :], in1=zsT[:, hu, qsl], op=ALU.mult), [("ft", fj), zkey], [okey])
                pending.extend([s_z])

        s_emitted = 0
        for i in range(NIT):
            look = 1 if blocks[i // 32][1] == "A" else 2
            while s_emitted < min(NIT, i + 1 + look):
                nsrc = blocks[s_emitted // 32][3]
                if nsrc not in loaded:
                    load_kv(nsrc)
                if blocks[s_emitted // 32][1] == "A" and s_emitted > i + 1:
                    break
                emit_S(s_emitted)
                s_emitted += 1
            emit_PV(i)
            if i % 32 == 31:
                emit_finish(i)
            elif pending and i % 32 >= 2:
                pending.popleft()()
        while pending:
            pending.popleft()()
        T.barrier()
        if debug == 3:
            dtmp = view(arena, 0, [128, NOWN], F32)
            for hu in range(16):
                T.op("dve", lambda e, hu=hu: e.tensor_copy(out=dtmp, in_=ogT[:, hu, :]), [], ["dtmp"])
                T.dma("sp", dbg["og"][:, hu, :], dtmp, ["dtmp"], [], "d_dbg")
            T.barrier()
            return nc

        mT = view(arena, 0, [128, 16, NOWN], BF16)
        gsl = [[view(big, (s * 2 + g) * 8, [128, KC, 256], BF16) for g in range(2)] for s in range(2)]
        wo_ab = view(big, 32, [128, 2, 2, 8, 256], BF16)
        lnb = view(big, 48, [128, 2, 1024], F32)
        rsb = view(big, 56, [128, 2, 1024], F32)

        def load_p4(db):
            s_ = db % 2
            load_wblock(gsl[s_][0], w_in, C_GA + db * 256, 256, ("gsl", s_, 0), "d_gsl%d0" % s_)
            load_wblock(gsl[s_][1], w_in, C_GB + db * 256, 256, ("gsl", s_, 1), "d_gsl%d1" % s_)
            T.dma("pool", wo_ab[:, s_, 0], w_oa[:, db * 256:(db + 1) * 256].rearrange("(f p) n -> p f n", p=128), [], [("woab", s_, 0)], "d_woa%d" % s_)
            T.dma("pool", wo_ab[:, s_, 1], w_ob[:, db * 256:(db + 1) * 256].rearrange("(f p) n -> p f n", p=128), [], [("woab", s_, 1)], "d_wob%d" % s_)

        load_p4(0)
        for db in range(8):
            s = db % 2
            for c2 in range(2):
                dc = db * 2 + c2
                csl = slice(c2 * 128, (c2 + 1) * 128)
                for tc in range(2):
                    if c2 == 0 and tc == 1 and db + 1 < 8:
                        load_p4(db + 1)
                    tsl = slice(tc * CH, (tc + 1) * CH)
                    hk = [("hT_own", 0), ("hT_own", 1)]
                    pg, py = 2 * tc, 2 * tc + 1
                    proj_fm((pg, 0), lambda kc: gsl[s][0][:, kc, csl], lambda kc: hT_own[:, kc, tsl], hk + [("gsl", s, 0)])
                    proj_fm((pg, 1), lambda kc: gsl[s][1][:, kc, csl], lambda kc: hT_own[:, kc, tsl], hk + [("gsl", s, 1)])

                    def ya(e, br_, pb):
                        last = None
                        for fc in range(8):
                            last = e.matmul(psa(*pb), lhsT=wo_ab[:, s, br_, fc, csl], rhs=ogT[:, br_ * 8 + fc, tsl], start=(fc == 0), stop=(fc == 7))
                        return last
                    ogk = [("ogT", hh, tc) for hh in range(16)]
                    T.op("pe", lambda e: ya(e, 0, (py, 0)), ogk + [("woab", s, 0)], [psk(py, 0)])
                    T.op("pe", lambda e: ya(e, 1, (py, 1)), ogk + [("woab", s, 1)], [psk(py, 1)])
                    j = tc
                    T.op("act", lambda e: e.activation(out=lnb[:, j, :], in_=PS[pg][:], func=AF.Sigmoid), [psk(pg, 0), psk(pg, 1)], [("lnb", j)])
                    T.op("dve", lambda e: e.tensor_tensor(out=rsb[:, j, :], in0=PS[py][:], in1=lnb[:, j, :], op=ALU.mult), [psk(py, 0), psk(py, 1), ("lnb", j)], [("rsb", j)])
                    T.op("pool", lambda e: e.tensor_tensor(out=mT[:, dc, tsl], in0=rsb[:, j, 0:512], in1=rsb[:, j, 512:1024], op=ALU.add), [("rsb", j)], [("mT", dc, tc)])
        T.barrier()
        if debug == 4:
            dtmp = view(big, 0, [128, NOWN], F32)
            for hu in range(16):
                T.op("dve", lambda e, hu=hu: e.tensor_copy(out=dtmp, in_=mT[:, hu, :]), [], ["dtmp"])
                T.dma("sp", dbg["mg"][:, hu, :], dtmp, ["dtmp"], [], "d_dbg")
            T.barrier()
            return nc

        wo_sb = big[:, 0:32768].rearrange("p (k n) -> p k n", k=KC)
        junk5 = view(big, 64, [128, D], BF16)
        xs5 = view(arena, 32, [128, 2, D], F32)
        gbc = view(arena, 48, [128, D], F32)
        yb = bufB[:, 0:8192].bitcast(F32).rearrange("p (s n) -> p s n", s=2)
        for cb in range(4):
            T.dma("pool", wo_sb[:, :, cb * 512:(cb + 1) * 512], w_o[:, cb * 512:(cb + 1) * 512].rearrange("(k p) n -> p k n", p=128), [], [("wo", cb)], "d_wo%d" % cb)
        T.dma("sp", gbc, final_g.partition_broadcast(128), [], ["gbc"], "d_gbc")
        for tt in range(8):
            xb = tt % 2
            xk = ("xs", xb)
            T.dma("sp", xs5[:, xb, :], xkv[tt * 128:(tt + 1) * 128, :], [], [xk], "d_xs%d" % xb)
            yk = ("yb", xb)
            for cb in range(4):
                pb = (cb // 2, cb % 2) if tt % 2 == 0 else (2 + cb // 2, cb % 2)

                def f(e):
                    last = None
                    for kc in range(KC):
                        last = e.matmul(psa(*pb), lhsT=mT[:, kc, tt * 128:(tt + 1) * 128], rhs=wo_sb[:, kc, cb * 512:(cb + 1) * 512], start=(kc == 0), stop=(kc == KC - 1))
                    return last
                T.op("pe", f, [("mT", kc, tt // 4) for kc in range(KC)] + [("wo", cb)], [psk(*pb)])
                T.op("dve", lambda e: e.tensor_tensor(out=yb[:, xb, cb * 512:(cb + 1) * 512], in0=psa(*pb), in1=xs5[:, xb, cb * 512:(cb + 1) * 512], op=ALU.add),
                     [psk(*pb), xk], [yk])
            rstd, sk = rms_stats(yb[:, xb, :], yk, xb, junk5, "junk5")
            T.op("dve", lambda e: e.scalar_tensor_tensor(out=xs5[:, xb, :], in0=yb[:, xb, :], scalar=rstd, in1=gbc, op0=ALU.mult, op1=ALU.mult), [yk, sk, "gbc"], [xk])
            T.dma("sp", out[tt * 128:(tt + 1) * 128, :], xs5[:, xb, :], [xk], [], "d_out%d" % xb)
        T.barrier()
    return nc


def _rope_tables():
    inv = (1.0 / (10000.0 ** (np.arange(0, 64, 2, dtype=np.float32) / np.float32(64)))).astype(np.float32)
    pos = np.arange(S, dtype=np.float32)
    angA = pos[:, None] * inv[None, :]
    angR = (np.arange(S) // 64).astype(np.float32)[:, None] * inv[None, :]
    angC = (np.arange(S) % 64).astype(np.float32)[:, None] * inv[None, :]
    p = np.arange(128)
    f = p % 32
    sign = np.where((p % 64) < 32, -1.0, 1.0).astype(np.float32)
    cosA = np.cos(angA)[:, f].T.astype(np.float32)
    sinA = (np.sin(angA)[:, f].T * sign[:, None]).astype(np.float32)
    angB = np.where((p < 64)[None, :], angR[:, f], angC[:, f])
    cosB = np.cos(angB).T.astype(np.float32)
    sinB = (np.sin(angB).T * sign[:, None]).astype(np.float32)
    return cosA, sinA, cosB, sinB


def _const_mats():
    ident = np.eye(128, dtype=np.float32)
    p = np.arange(128)
    partner = np.where((p % 64) < 32, p + 32, p - 32)
    perm = np.zeros((128, 128), np.float32)
    perm[partner, p] = 1.0
    ones = np.ones((128, 128), np.float32)
    return np.ascontiguousarray(np.stack([ident, perm, ones], axis=1))


def make_in_maps(x, norm_g, w_in, lambda_q1, lambda_k1, lambda_q2, lambda_k2, subln_g, q_norm_g, k_norm_g,
                 w_out_a, w_out_b, w_o, final_g):
    f32 = lambda a: np.ascontiguousarray(np.asarray(a, dtype=np.float32))
    x = f32(x)
    tabs = _rope_tables()
    cm = _const_mats()
    lam4 = f32(np.concatenate([np.asarray(lambda_q1)[0], np.asarray(lambda_k1)[0], np.asarray(lambda_q2)[0], np.asarray(lambda_k2)[0]]))
    gvecs = f32(np.stack([np.asarray(subln_g)[0], np.asarray(q_norm_g)[0], np.asarray(k_norm_g)[0]], axis=1))
    w_in0, woa0, wob0, wo0 = f32(np.asarray(w_in)[0]), f32(np.asarray(w_out_a)[0]), f32(np.asarray(w_out_b)[0]), f32(np.asarray(w_o)[0])
    ng, fg = f32(np.asarray(norm_g)[0]), f32(final_g)
    maps = []
    for core in range(8):
        b, c = core // 4, core % 4
        order = np.roll(np.arange(S), -c * NOWN)
        m = {"xkv": np.ascontiguousarray(x[b][order]), "w_in": w_in0, "w_out_a": woa0, "w_out_b": wob0, "w_o": wo0,
             "norm_g": ng, "final_g": fg, "lam4": lam4, "gvecs": gvecs, "cmats": cm}
        for n, t in zip(("cosA", "sinA", "cosB", "sinB"), tabs):
            m[n] = np.ascontiguousarray(t[:, order])
        maps.append(m)
    return maps


def kernel(**inputs):
    nc = build_nc()
    maps = make_in_maps(**inputs)
    res = run_bass_kernel_spmd(nc, maps, core_ids=list(range(8)))
    out = np.empty((2, S, D), np.float32)
    for core in range(8):
        b, c = core // 4, core % 4
        out[b, c * NOWN:(c + 1) * NOWN] = res.results[core]["out"]
    return out
```

```python
import math
from collections import deque
from contextlib import ExitStack

import numpy as np
import concourse.bass as bass
import concourse.mybir as mybir
from concourse.bass_utils import run_bass_kernel_spmd

F32 = mybir.dt.float32
BF16 = mybir.dt.bfloat16
AF = mybir.ActivationFunctionType
ALU = mybir.AluOpType

D = 2048
KC = 16
S = 4096
NOWN = 1024
CH = 512
NCH = S // CH
COLS = 10752
EPS = 1e-6
LAM_INIT = 0.8 - 0.6 * math.exp(-0.3 * 0)
C_QA, C_KA, C_VA, C_ZA, C_QB, C_KB, C_VB, C_ZB, C_GA, C_GB = 0, 1024, 2048, 3072, 4096, 5120, 5376, 5632, 6656, 8704
SC_A = 64 ** -0.5
SC_B = 128 ** -0.5


class Tracker:
    def __init__(self, nc, stack):
        self.nc = nc
        self.stack = stack
        self.eng = {"pe": nc.tensor, "act": nc.scalar, "dve": nc.vector, "pool": nc.gpsimd, "sp": nc.sync}
        self.sems = {}
        self.cnt = {}
        self.waited = {e: {} for e in self.eng}
        self.lastw = {}
        self.readers = {}
        for e in self.eng:
            self._sem("e_" + e)

    def _sem(self, name):
        if name not in self.sems:
            self.sems[name] = self.stack.enter_context(self.nc.semaphore(name))
            self.cnt[name] = 0
        return self.sems[name]

    def _wait_for(self, e, toks):
        need = {}
        for t in toks:
            if t is None:
                continue
            s, v = t
            if v > need.get(s, 0):
                need[s] = v
        for s, v in need.items():
            if self.waited[e].get(s, 0) >= v:
                continue
            if e == "pe" and s == "e_pe":
                continue
            self.eng[e].wait_ge(self.sems[s], v)
            self.waited[e][s] = v

    def _deps(self, reads, writes, e=None):
        toks = []
        for k in reads:
            toks.append(self.lastw.get(k))
        own = "e_" + e if e else None
        for k in writes:
            for t in [self.lastw.get(k)] + list(self.readers.get(k, ())):
                if t is not None and t[0] != own:
                    toks.append(t)
        return toks

    def _record(self, tok, reads, writes):
        for k in reads:
            self.readers.setdefault(k, []).append(tok)
        for k in writes:
            self.lastw[k] = tok
            self.readers[k] = []

    @staticmethod
    def _excl(reads, writes):
        ps_reads = [k for k in reads if isinstance(k, tuple) and k and k[0] == "ps"]
        if not ps_reads:
            return reads, writes
        return [k for k in reads if k not in ps_reads], list(writes) + ps_reads

    def op(self, e, fn, reads=(), writes=()):
        reads, writes = self._excl(reads, writes)
        self._wait_for(e, self._deps(reads, writes, e))
        ins = fn(self.eng[e])
        s = "e_" + e
        self.cnt[s] += 1
        ins.then_inc(self.sems[s], 1)
        self._record((s, self.cnt[s]), reads, writes)

    def dma(self, q, out, in_, reads, writes, sem):
        self._sem(sem)
        self._wait_for(q, self._deps(reads, writes))
        ins = self.eng[q].dma_start(out=out, in_=in_)
        self.cnt[sem] += 16
        ins.then_inc(self.sems[sem], 16)
        self._record((sem, self.cnt[sem]), reads, writes)

    def barrier(self):
        toks = [(s, c) for s, c in self.cnt.items() if c > 0]
        for e in self.eng:
            self._wait_for(e, toks)
        self.lastw.clear()
        self.readers.clear()


def build_nc(debug=None):
    nc = bass.Bass("TRN2", target_bir_lowering=False)
    dt = nc.dram_tensor
    xkv = dt("xkv", [S, D], F32, kind="ExternalInput").ap()
    w_in = dt("w_in", [D, COLS], F32, kind="ExternalInput").ap()
    w_oa = dt("w_out_a", [1024, D], F32, kind="ExternalInput").ap()
    w_ob = dt("w_out_b", [1024, D], F32, kind="ExternalInput").ap()
    w_o = dt("w_o", [D, D], F32, kind="ExternalInput").ap()
    norm_g = dt("norm_g", [D], F32, kind="ExternalInput").ap()
    final_g = dt("final_g", [D], F32, kind="ExternalInput").ap()
    lam4 = dt("lam4", [256], F32, kind="ExternalInput").ap()
    gvecs = dt("gvecs", [128, 3], F32, kind="ExternalInput").ap()
    tabs = {n: dt(n, [128, S], F32, kind="ExternalInput").ap() for n in ("cosA", "sinA", "cosB", "sinB")}
    cmats = dt("cmats", [128, 3, 128], F32, kind="ExternalInput").ap()
    out = dt("out", [NOWN, D], F32, kind="ExternalOutput").ap()
    kT_scr = dt("kT_scr", [10, 128, S], BF16, kind="Internal").ap()
    v_scr = dt("v_scr", [10, 128, 32, 128], BF16, kind="Internal").ap()
    dbg = {}
    if debug:
        dbg["qT"] = dt("dbg_qT", [128, 16, NOWN], F32, kind="ExternalOutput").ap()
        dbg["zs"] = dt("dbg_zs", [128, 16, NOWN], F32, kind="ExternalOutput").ap()
        dbg["kT"] = dt("dbg_kT", [10, 128, S], BF16, kind="ExternalOutput").ap()
        dbg["v"] = dt("dbg_v", [10, 128, 32, 128], BF16, kind="ExternalOutput").ap()
        dbg["og"] = dt("dbg_og", [128, 16, NOWN], F32, kind="ExternalOutput").ap()
        dbg["mg"] = dt("dbg_mg", [128, 16, NOWN], F32, kind="ExternalOutput").ap()

    with ExitStack() as st:
        E = st.enter_context
        T = Tracker(nc, st)
        sb = lambda name, shape, dtp: E(nc.sbuf_tensor(name, shape, dtp))

        big = sb("big", [128, 40960], BF16)
        hT_own = sb("hT_own", [128, KC, NOWN], BF16)
        bufB = sb("bufB", [128, 16384], BF16)
        arena = sb("arena", [128, 29184], BF16)
        cm = sb("cm", [128, 3, 128], BF16)
        gv = sb("gv", [128, 3], F32)
        g16 = sb("g16", [128, KC], F32)
        lamb = sb("lamb", [128, 256], F32)
        lamt = sb("lamt", [128, 8], F32)
        epsb = sb("epsb", [128, 1], F32)
        st8 = sb("st8", [128, 16], F32)

        def view(buf, off_kib, shape, dtp):
            esz = 4 if dtp == F32 else 2
            n = 1
            for s_ in shape[1:]:
                n *= s_
            a_ = int(off_kib * 1024) // 2
            ap = buf[:, a_:a_ + n * esz // 2]
            if dtp == F32:
                ap = ap.bitcast(F32)
            if len(shape) == 3:
                ap = ap.rearrange("p (a b) -> p a b", a=shape[1])
            elif len(shape) == 4:
                ap = ap.rearrange("p (a b c) -> p a b c", a=shape[1], b=shape[2])
            elif len(shape) == 5:
                ap = ap.rearrange("p (a b c d) -> p a b c d", a=shape[1], b=shape[2], c=shape[3])
            return ap

        xs = view(arena, 0, [128, 2, D], F32)
        hb2 = view(arena, 16, [128, 2, D], BF16)
        tab = view(arena, 24, [128, 4, CH], F32)
        kb = view(arena, 32, [128, 2, CH], BF16)
        t1 = view(arena, 34, [128, 2, CH], F32)
        t2 = view(arena, 38, [128, 2, CH], F32)
        lnb = view(arena, 42, [128, 2, CH], F32)
        rsb = view(arena, 46, [128, 2, CH], F32)
        sqb = view(arena, 50, [128, 2, CH], BF16)
        kout = view(arena, 52, [128, 2, CH], BF16)
        vout = view(arena, 54, [128, 1280], BF16)
        PS = [E(nc.psum_tensor("ps%d" % i, [128, 1024], F32)) for i in range(4)]

        ident = cm[:, 0, :]
        perm = cm[:, 1, :]
        ones = cm[:, 2, :]

        def psk(i, h):
            return ("ps", i, h)

        def psa(i, h):
            return PS[i][:, h * 512:(h + 1) * 512]

        T.dma("pool", cm[:], cmats, [], ["cm"], "d_cm")
        T.dma("sp", gv[:], gvecs, [], ["gv"], "d_gv")
        T.dma("sp", lamb[:], lam4.partition_broadcast(128), [], ["lamb"], "d_lam")
        with nc.allow_non_contiguous_dma(reason="tiny gain vector transpose load"):
            T.dma("sp", g16[:], norm_g.rearrange("(k p) -> p k", p=128), [], ["g16"], "d_g16")
        T.op("dve", lambda e: e.memset(epsb[:], EPS), [], ["epsb"])
        T.op("dve", lambda e: e.tensor_tensor(out=t1[:, 0, 0:64], in0=lamb[:, 0:64], in1=lamb[:, 64:128], op=ALU.mult), ["lamb"], ["t1c"])
        T.op("dve", lambda e: e.tensor_tensor(out=t1[:, 0, 64:128], in0=lamb[:, 128:192], in1=lamb[:, 192:256], op=ALU.mult), ["lamb", "t1c"], ["t1c"])
        T.op("dve", lambda e: e.reduce_sum(out=lamt[:, 0:1], in_=t1[:, 0, 0:64], axis=mybir.AxisListType.X), ["t1c"], ["lamt"])
        T.op("dve", lambda e: e.reduce_sum(out=lamt[:, 1:2], in_=t1[:, 0, 64:128], axis=mybir.AxisListType.X), ["t1c", "lamt"], ["lamt"])
        T.op("act", lambda e: e.activation(out=lamt[:, 2:4], in_=lamt[:, 0:2], func=AF.Exp), ["lamt"], ["lamt"])
        T.op("dve", lambda e: e.scalar_tensor_tensor(out=lamt[:, 4:5], in0=lamt[:, 3:4], scalar=-LAM_INIT, in1=lamt[:, 2:3], op0=ALU.add, op1=ALU.subtract), ["lamt"], ["lamt"])
        T.op("dve", lambda e: e.tensor_scalar(out=lamt[:, 5:6], in0=gv[:, 0:1], scalar1=(1.0 - LAM_INIT), scalar2=None, op0=ALU.mult), ["gv", "lamt"], ["lamt"])
        neglam = lamt[:, 4:5]
        gsub = lamt[:, 5:6]
        T.barrier()
        if debug == 0.1:
            return nc

        def load_wblock(dst3, src_w, col0, ncols, key, sem):
            T.dma("pool", dst3, src_w[:, col0:col0 + ncols].rearrange("(kc p) n -> p kc n", p=128), [], [key], sem)

        def rope_finish(src_ap, src_keys, j, cos_ap, sin_ap, tab_keys, out_ap, out_key, rot_ps, inplace=False):
            ri, rh = rot_ps
            T.op("act", lambda e: e.activation(out=kb[:, j, :], in_=src_ap, func=AF.Copy), src_keys, [("kb", j)])
            T.op("pe", lambda e: e.matmul(psa(ri, rh), lhsT=perm, rhs=kb[:, j, :], start=True, stop=True), [("kb", j), "cm"], [psk(ri, rh)])
            T.op("dve", lambda e: e.tensor_tensor(out=t1[:, j, :], in0=src_ap, in1=cos_ap, op=ALU.mult), list(src_keys) + list(tab_keys), [("t1", j)])
            T.op("dve", lambda e: e.tensor_tensor(out=t2[:, j, :], in0=psa(ri, rh), in1=sin_ap, op=ALU.mult), [psk(ri, rh)] + list(tab_keys), [("t2", j)])
            T.op("pool", lambda e: e.tensor_tensor(out=out_ap, in0=t1[:, j, :], in1=t2[:, j, :], op=ALU.add), [("t1", j), ("t2", j)], [out_key])

        def headnorm(src_ps, src_key, j, gcol, aux_ps):
            ai, ah = aux_ps
            T.op("act", lambda e: e.activation(out=sqb[:, j, :], in_=src_ps, func=AF.Square), [src_key], [("sqb", j)])
            T.op("pe", lambda e: e.matmul(psa(ai, ah), lhsT=ones, rhs=sqb[:, j, :], start=True, stop=True), [("sqb", j), "cm"], [psk(ai, ah)])
            T.op("act", lambda e: e.activation(out=lnb[:, j, 0:CH], in_=psa(ai, ah), func=AF.Ln, scale=1.0 / 128, bias=epsb[:]), [psk(ai, ah), "epsb"], [("lnb", j)])
            T.op("act", lambda e: e.activation(out=rsb[:, j, 0:CH], in_=lnb[:, j, 0:CH], func=AF.Exp, scale=-0.5), [("lnb", j)], [("rsb", j)])
            T.op("dve", lambda e: e.scalar_tensor_tensor(out=t1[:, j, :], in0=src_ps, scalar=gv[:, gcol:gcol + 1], in1=rsb[:, j, 0:CH], op0=ALU.mult, op1=ALU.mult), [src_key, ("rsb", j), "gv"], [("t1", j)])

        def proj_fm(ps_idx, wfun, rfun, reads):
            pi, ph = ps_idx

            def f(e):
                last = None
                for kc in range(KC):
                    last = e.matmul(psa(pi, ph), lhsT=wfun(kc), rhs=rfun(kc), start=(kc == 0), stop=(kc == KC - 1))
                return last
            T.op("pe", f, reads, [psk(pi, ph)])

        def rms_stats(src_ap, src_key, slot, junk_ap, junk_key):
            sc = st8[:, 4 * slot:4 * slot + 4]
            sk = ("st8", slot)
            T.op("act", lambda e: e.activation(out=junk_ap, in_=src_ap, func=AF.Square, accum_out=sc[:, 0:1]), [src_key], [junk_key, sk])
            T.op("act", lambda e: e.activation(out=sc[:, 1:2], in_=sc[:, 0:1], func=AF.Ln, scale=1.0 / D, bias=epsb[:]), [sk, "epsb"], [sk])
            T.op("act", lambda e: e.activation(out=sc[:, 2:3], in_=sc[:, 1:2], func=AF.Exp, scale=-0.5), [sk], [sk])
            return sc[:, 2:3], sk

        dq = []
        tickc = [0]

        def defer(n, fn):
            dq.append([tickc[0] + n, fn])

        def tick():
            tickc[0] += 1
            for d_ in [d_ for d_ in dq if d_[0] <= tickc[0]]:
                dq.remove(d_)
                d_[1]()

        def flush():
            while dq:
                dq.pop(0)[1]()

        wkv = big[:, 0:KC * 2560].rearrange("p (k n) -> p k n", k=KC)
        wkv_blocks = [(0, C_KA, 512), (512, C_KA + 512, 512), (1024, C_KB, 256),
                      (1280, C_VA, 512), (1792, C_VA + 512, 512), (2304, C_VB, 256)]
        for bi, (dcol, scol, n) in enumerate(wkv_blocks):
            load_wblock(wkv[:, :, dcol:dcol + n], w_in, scol, n, ("wkv", bi), "d_wkv%d" % bi)
        wkv_keys = [("wkv", bi) for bi in range(6)]
        if debug == 0.2:
            T.barrier()
            return nc
        hT_rot = bufB[:].rearrange("p (b k t) -> p b k t", b=2, k=KC)
        kcount = 0

        def tile_dst(ch, t4):
            if ch < 2:
                return hT_own, ("hT_own", ch), ch * CH + t4 * 128
            return hT_rot[:, ch % 2], ("hT_rot", ch % 2), t4 * 128

        def prepX(tt):
            if tt >= 32:
                return
            xbuf = tt % 2
            T.dma("sp", xs[:, xbuf, :], xkv[tt * 128:(tt + 1) * 128, :], [], [("xs", xbuf)], "d_xs%d" % xbuf)

        def prepA(ch, t4):
            tt = ch * 4 + t4
            xbuf = tt % 2
            xk = ("xs", xbuf)
            rstd, sk = rms_stats(xs[:, xbuf, :], xk, xbuf, hb2[:, xbuf, :], ("hb", xbuf))
            T.op("act", lambda e: e.activation(out=hb2[:, xbuf, :], in_=xs[:, xbuf, :], func=AF.Copy, scale=rstd), [xk, sk], [("hb", xbuf)])
            prepX(tt + 2)

        def prepB(ch, t4):
            tt = ch * 4 + t4
            xbuf = tt % 2
            dstT, dkey, tok_off = tile_dst(ch, t4)
            pst = PS[3][:].bitcast(BF16)

            def tr(e):
                last = None
                for kc in range(KC):
                    last = e.transpose(pst[:, kc * 128:(kc + 1) * 128], hb2[:, xbuf, kc * 128:(kc + 1) * 128], ident)
                return last
            T.op("pe", tr, [("hb", xbuf), "cm"], [psk(3, 0), psk(3, 1)])
            for kc in range(KC):
                T.op("dve", lambda e, kc=kc: e.tensor_scalar(out=dstT[:, kc, tok_off:tok_off + 128], in0=pst[:, kc * 128:(kc + 1) * 128],
                                                             scalar1=g16[:, kc:kc + 1], scalar2=None, op0=ALU.mult),
                     [psk(3, 0), psk(3, 1), "g16"], [dkey])

        def load_tabs(ch, which):
            for ti in which:
                tn = ("cosA", "sinA", "cosB", "sinB")[ti]
                T.dma("sp", tab[:, ti, :], tabs[tn][:, ch * CH:(ch + 1) * CH], [], [("tab", ti)], "d_tab%d" % ti)

        load_tabs(0, (0, 1, 2, 3))
        prepX(0)
        prepX(1)
        if debug == 0.31:
            prepA(0, 0)
            T.barrier()
            return nc
        for t4 in range(4):
            prepA(0, t4)
            prepB(0, t4)
        if debug == 0.3:
            T.barrier()
            return nc
        for ch in range(NCH):
            if ch < 2:
                hT = hT_own[:, :, ch * CH:(ch + 1) * CH]
                hkey = ("hT_own", ch)
            else:
                hT = hT_rot[:, ch % 2, :, :]
                hkey = ("hT_rot", ch % 2)
            for cc in range(10):
                pb = (cc % 2, 0)
                proj_fm(pb, lambda kc, cc=cc: wkv[:, kc, cc * 128:(cc + 1) * 128], lambda kc: hT[:, kc, :], [hkey, ("wkv", 0 if cc < 4 else (1 if cc < 8 else 2))])
                tick()
                if ch + 1 < NCH and cc % 2 == 0:
                    if cc // 2 < 4:
                        prepA(ch + 1, cc // 2)
                    if 1 <= cc // 2 <= 4:
                        prepB(ch + 1, cc // 2 - 1)
                j = cc % 2
                ko = kcount % 2
                kcount += 1

                def store(cc=cc, ko=ko, ch=ch):
                    T.dma("pool", kT_scr[cc, :, ch * CH:(ch + 1) * CH], kout[:, ko, :], [("kout", ko)], [], "d_kout%d" % ko)

                if cc < 8:
                    def postA(cc=cc, pb=pb, j=j, ko=ko, ch=ch, store=store):
                        rope_finish(psa(*pb), [psk(*pb)], j, tab[:, 0, :], tab[:, 1, :], [("tab", 0), ("tab", 1)], kout[:, ko, :], ("kout", ko), (cc % 2, 1))
                        store()
                        if cc == 7 and ch + 1 < NCH:
                            load_tabs(ch + 1, (0, 1))
                    defer(1, postA)
                else:
                    def postB1(cc=cc, pb=pb, j=j):
                        headnorm(psa(*pb), psk(*pb), j, 2, (cc % 2, 1))

                    def postB2(cc=cc, pb=pb, j=j, ko=ko, ch=ch, store=store):
                        rope_finish(t1[:, j, :], [("t1", j)], j, tab[:, 2, :], tab[:, 3, :], [("tab", 2), ("tab", 3)], kout[:, ko, :], ("kout", ko), (cc % 2, 1))
                        store()
                        if cc == 9 and ch + 1 < NCH:
                            load_tabs(ch + 1, (2, 3))
                    defer(1, postB1)
                    defer(2, postB2)
            for t4 in range(4):
                tt = ch * 4 + t4
                vkey = "vout"
                for blk, (c0, n) in enumerate(((1280, 512), (1792, 512), (2304, 256))):
                    pb = (2, blk % 2)

                    def f(e, c0=c0, n=n, pb=pb, t4=t4):
                        last = None
                        for kc in range(KC):
                            last = e.matmul(psa(*pb)[:, 0:n], lhsT=hT[:, kc, t4 * 128:(t4 + 1) * 128], rhs=wkv[:, kc, c0:c0 + n], start=(kc == 0), stop=(kc == KC - 1))
                        return last
                    T.op("pe", f, [hkey, ("wkv", 3 + blk)], [psk(*pb)])
                    tick()
                    T.op("act", lambda e, c0=c0, n=n, pb=pb: e.activation(out=vout[:, c0 - 1280:c0 - 1280 + n], in_=psa(*pb)[:, 0:n], func=AF.Copy), [psk(*pb)], [vkey])
                T.dma("act", v_scr[:, :, tt, :].rearrange("h p d -> p h d"), vout[:].rearrange("p (h d) -> p h d", h=10), [vkey], [], "d_vout")
        flush()
        T.barrier()

        if debug == 1:
            T.dma("sp", dbg["kT"], kT_scr, [], [], "d_dbg")
            T.dma("sp", dbg["v"], v_scr, [], [], "d_dbg")
            T.barrier()
            return nc

        qT = big[:, 0:16384].rearrange("p (k t) -> p k t", k=16)
        zsT = big[:, 16384:32768].rearrange("p (k t) -> p k t", k=16)
        tab2 = view(big, 64, [128, 4, NOWN], F32)
        wst3 = [bufB[:, s * 8192:(s + 1) * 8192].rearrange("p (k n) -> p k n", k=KC) for s in range(2)]
        q_blocks = [("qA", C_QA), ("qA", C_QA + 512), ("zA", C_ZA), ("zA", C_ZA + 512),
                    ("qB", C_QB), ("qB", C_QB + 512), ("zB", C_ZB), ("zB", C_ZB + 512)]
        if debug in (1.5, 1.6, 1.7):
            q_blocks = {1.5: q_blocks[0:1], 1.6: q_blocks[2:3], 1.7: q_blocks[4:5]}[debug]
        load_wblock(wst3[0], w_in, q_blocks[0][1], 512, ("wst", 0), "d_wst0")
        for ti, tn in enumerate(("cosA", "sinA", "cosB", "sinB")):
            T.dma("sp", tab2[:, ti, :], tabs[tn][:, 0:NOWN], [], [("tab2", ti)], "d_tab%d" % ti)
        cnt = 0
        for bi, (kind, col0) in enumerate(q_blocks):
            slot = bi % 2
            for c4 in range(4):
                hu_local = (bi % 2) * 4 + c4
                for tc in range(2):
                    if c4 == 1 and tc == 0 and bi + 1 < len(q_blocks):
                        load_wblock(wst3[(bi + 1) % 2], w_in, q_blocks[bi + 1][1], 512, ("wst", (bi + 1) % 2), "d_wst%d" % ((bi + 1) % 2))
                    pb = (cnt % 2, 0)
                    aux = (cnt % 2, 1)
                    j = cnt % 2
                    cnt += 1
                    proj_fm(pb, lambda kc, c4=c4, slot=slot: wst3[slot][:, kc, c4 * 128:(c4 + 1) * 128],
                            lambda kc, tc=tc: hT_own[:, kc, tc * CH:(tc + 1) * CH], [("hT_own", 0), ("hT_own", 1), ("wst", slot)])
                    tsl = slice(tc * CH, (tc + 1) * CH)
                    tick()
                    if kind == "qA":
                        def postA(pb=pb, j=j, tsl=tsl, hu_local=hu_local, tc=tc, aux=aux):
                            rope_finish(psa(*pb), [psk(*pb)], j, tab2[:, 0, tsl], tab2[:, 1, tsl], [("tab2", 0), ("tab2", 1)], qT[:, hu_local, tsl], ("qT", hu_local, tc), aux)
                        defer(1, postA)
                    elif kind == "qB":
                        def postB1(pb=pb, j=j, aux=aux):
                            headnorm(psa(*pb), psk(*pb), j, 1, aux)

                        def postB2(pb=pb, j=j, tsl=tsl, hu_local=hu_local, tc=tc, aux=aux):
                            rope_finish(t1[:, j, :], [("t1", j)], j, tab2[:, 2, tsl], tab2[:, 3, tsl], [("tab2", 2), ("tab2", 3)], qT[:, 8 + hu_local, tsl], ("qT", 8 + hu_local, tc), aux)
                        defer(1, postB1)
                        defer(2, postB2)
                    else:
                        hu = hu_local + (0 if kind == "zA" else 8)
                        T.op("act", lambda e, hu=hu, tsl=tsl, pb=pb: e.activation(out=zsT[:, hu, tsl], in_=psa(*pb), func=AF.Silu), [psk(*pb)], [("zsT", hu, tc)])
        flush()
        T.barrier()
        if debug in (1.5, 1.6, 1.7):
            return nc
        if debug == 2:
            dtmp = view(arena, 0, [128, NOWN], F32)
            for hu in range(16):
                T.op("dve", lambda e, hu=hu: e.tensor_copy(out=dtmp, in_=qT[:, hu, :]), [], ["dtmp"])
                T.dma("sp", dbg["qT"][:, hu, :], dtmp, ["dtmp"], [], "d_dbg")
                T.op("dve", lambda e, hu=hu: e.tensor_copy(out=dtmp, in_=zsT[:, hu, :]), [], ["dtmp"])
                T.dma("sp", dbg["zs"][:, hu, :], dtmp, ["dtmp"], [], "d_dbg")
            T.barrier()
            return nc

        ogT = bufB[:].rearrange("p (k t) -> p k t", k=16)
        kvb = view(arena, 0, [128, 2, 8192], BF16)
        fa = view(arena, 32, [128, 1, CH], F32)
        fb = view(arena, 34, [128, 1, CH], F32)
        fo = view(arena, 36, [128, 1, CH], F32)
        ft = view(arena, 38, [128, 1, CH], F32)
        lnb = view(arena, 40, [128, 1, 1024], F32)
        rsb = view(arena, 44, [128, 1, 1024], F32)
        sqb = view(arena, 48, [128, 1, CH], BF16)
        pT = view(big, 64, [128, 4, 1024], BF16)
        tsum = view(arena, 50, [128, 2, 1024], BF16)
        units = [("A", h, h) for h in range(8)] + [("B", h, 8 + h // 4) for h in range(8)]
        loaded = {}
        nload = [0]

        def load_kv(src_):
            slot_ = nload[0] % 2
            nload[0] += 1
            T.dma("sp", kvb[:, slot_, 0:4096], kT_scr[src_], [], [("kvK", slot_)], "d_kvK%d" % slot_)
            T.dma("sp", kvb[:, slot_, 4096:8192], v_scr[src_].rearrange("p k d -> p (k d)"), [], [("kvV", slot_)], "d_kvV%d" % slot_)
            loaded[src_] = slot_

        srcs = []
        for u in units:
            if u[2] not in srcs:
                srcs.append(u[2])
        load_kv(srcs[0])
        pending = deque()
        zdefer = []
        blocks = [(ui, br, h, src_, qb) for ui, (br, h, src_) in enumerate(units) for qb in range(2)]
        NIT = len(blocks) * 32

        def blk(i):
            ui, br, h, src_, qb = blocks[i // 32]
            kt = i % 32
            hu = h if br == "A" else 8 + h
            slot = loaded[src_]
            Kt = kvb[:, slot, 0:4096]
            Vt = kvb[:, slot, 4096:8192].rearrange("p (k d) -> p k d", k=32)
            return br, hu, src_, qb, kt, slot, Kt, Vt

        def pslot(i, br):
            if br == "A":
                return pT[:, i % 4, :], ("pT", i % 4, 0), ("pT", i % 4, 1)
            k = i % 8
            return pT[:, k // 2, (k % 2) * 512:(k % 2 + 1) * 512], ("pT", k // 2, k % 2), None

        def emit_S(i):
            br, hu, src_, qb, kt, slot, Kt, Vt = blk(i)
            sp_i = i % 2
            pap, pk0, pk1 = pslot(i, br)
            qsl = slice(qb * CH, (qb + 1) * CH)
            ksl = slice(kt * 128, (kt + 1) * 128)
            kK = ("kvK", slot)
            qkey = ("qT", hu, qb)
            if br == "A":
                def smm(e):
                    e.matmul(psa(sp_i, 0), lhsT=Kt[0:64, ksl], rhs=qT[0:64, hu, qsl], start=True, stop=True)
                    return e.matmul(psa(sp_i, 1), lhsT=Kt[64:128, ksl], rhs=qT[64:128, hu, qsl], start=True, stop=True)
                T.op("pe", smm, [kK, qkey], [psk(sp_i, 0), psk(sp_i, 1)])
                T.op("act", lambda e: e.activation(out=pap[:, 0:512], in_=psa(sp_i, 0), func=AF.Exp, scale=SC_A), [psk(sp_i, 0)], [pk0])
                T.op("act", lambda e: e.activation(out=pap[:, 512:1024], in_=psa(sp_i, 1), func=AF.Exp, scale=SC_A), [psk(sp_i, 1)], [pk1])
            else:
                sb_ = ((i % 4) // 2, (i % 4) % 2)
                T.op("pe", lambda e: e.matmul(psa(*sb_), lhsT=Kt[:, ksl], rhs=qT[:, hu, qsl], start=True, stop=True),
                     [kK, qkey], [psk(*sb_)])
                T.op("act", lambda e: e.activation(out=pap, in_=psa(*sb_), func=AF.Exp, scale=SC_B),
                     [psk(*sb_)], [pk0])

        def emit_PV(i):
            br, hu, src_, qb, kt, slot, Kt, Vt = blk(i)
            pap, pk0, pk1 = pslot(i, br)
            kV = ("kvV", slot)
            if kt == 0 and qb == 0:
                si = srcs.index(src_)
                if si + 1 < len(srcs) and srcs[si + 1] not in loaded:
                    load_kv(srcs[si + 1])
            if br == "A":
                T.op("pe", lambda e: e.matmul(psa(2, 0), lhsT=Vt[:, kt, :], rhs=pap[:, 0:512], start=(kt == 0), stop=(kt == 31)), [kV, pk0], [psk(2, 0)])
                T.op("pe", lambda e: e.matmul(psa(2, 1), lhsT=Vt[:, kt, :], rhs=pap[:, 512:1024], start=(kt == 0), stop=(kt == 31)), [kV, pk1], [psk(2, 1)])
            else:
                T.op("pe", lambda e: e.matmul(psa(2, 0), lhsT=Vt[:, kt, :], rhs=pap, start=(kt == 0), stop=(kt == 31)), [kV, pk0], [psk(2, 0)])
            if kt % 2 == 1:
                ts_ = (i // 2) % 2
                pap0, qk0, qk1 = pslot(i - 1, br)
                if br == "A":
                    T.op("dve", lambda e: e.tensor_tensor(out=tsum[:, ts_, :], in0=pap0, in1=pap, op=ALU.add), [qk0, qk1, pk0, pk1], [("tsum", ts_)])

                    def zmm():
                        def f(e):
                            e.matmul(psa(3, 0), lhsT=ones, rhs=tsum[:, ts_, 0:512], start=(kt == 1), stop=(kt == 31))
                            return e.matmul(psa(3, 1), lhsT=ones, rhs=tsum[:, ts_, 512:1024], start=(kt == 1), stop=(kt == 31))
                        T.op("pe", f, [("tsum", ts_), "cm"], [psk(3, 0), psk(3, 1)])
                else:
                    T.op("dve", lambda e: e.tensor_tensor(out=tsum[:, ts_, 0:512], in0=pap0, in1=pap, op=ALU.add), [qk0, pk0], [("tsum", ts_)])

                    def zmm():
                        T.op("pe", lambda e: e.matmul(psa(3, 0), lhsT=ones, rhs=tsum[:, ts_, 0:512], start=(kt == 1), stop=(kt == 31)), [("tsum", ts_), "cm"], [psk(3, 0)])
                if kt == 31:
                    zmm()
                else:
                    zdefer.append(zmm)
            elif zdefer:
                zdefer.pop(0)()

        def emit_finish(i):
            br, hu, src_, qb, kt, slot, Kt, Vt = blk(i)
            qsl = slice(qb * CH, (qb + 1) * CH)
            fj = 0
            okey = ("ogT", hu, qb)
            zkey = ("zsT", hu, qb)
            while pending:
                pending.popleft()()
            if br == "A":
                T.op("act", lambda e: e.activation(out=lnb[:, fj, :], in_=PS[3][:], func=AF.Ln), [psk(3, 0), psk(3, 1)], [("lnb", fj)])
                T.op("act", lambda e: e.activation(out=rsb[:, fj, :], in_=lnb[:, fj, :], func=AF.Exp, scale=-1.0), [("lnb", fj)], [("rsb", fj)])
                T.op("dve", lambda e: e.tensor_tensor(out=fa[:, fj, :], in0=psa(2, 0), in1=rsb[:, fj, 0:512], op=ALU.mult), [psk(2, 0), ("rsb", fj)], [("fa", fj)])
                T.op("dve", lambda e: e.tensor_tensor(out=fb[:, fj, :], in0=psa(2, 1), in1=rsb[:, fj, 512:1024], op=ALU.mult), [psk(2, 1), ("rsb", fj)], [("fb", fj)])

                def s_o():
                    T.op("dve", lambda e: e.scalar_tensor_tensor(out=fo[:, fj, :], in0=fb[:, fj, :], scalar=neglam, in1=fa[:, fj, :], op0=ALU.mult, op1=ALU.add),
                         [("fa", fj), ("fb", fj)], [("fo", fj)])

                def s_sq():
                    T.op("act", lambda e: e.activation(out=sqb[:, fj, :], in_=fo[:, fj, :], func=AF.Square), [("fo", fj)], [("sqb", fj)])

                def s_mmln():
                    T.op("pe", lambda e: e.matmul(psa(qb, 0), lhsT=ones, rhs=sqb[:, fj, :], start=True, stop=True), [("sqb", fj), "cm"], [psk(qb, 0)])
                    T.op("act", lambda e: e.activation(out=lnb[:, fj, 0:CH], in_=psa(qb, 0), func=AF.Ln, scale=1.0 / 128, bias=epsb[:]), [psk(qb, 0)], [("lnb", fj)])

                def s_ex():
                    T.op("act", lambda e: e.activation(out=rsb[:, fj, 0:CH], in_=lnb[:, fj, 0:CH], func=AF.Exp, scale=-0.5), [("lnb", fj)], [("rsb", fj)])

                def s_g():
                    T.op("dve", lambda e: e.scalar_tensor_tensor(out=ft[:, fj, :], in0=fo[:, fj, :], scalar=gsub, in1=rsb[:, fj, 0:CH], op0=ALU.mult, op1=ALU.mult),
                         [("fo", fj), ("rsb", fj)], [("ft", fj)])

                def s_z():
                    T.op("pool", lambda e: e.tensor_tensor(out=ogT[:, hu, qsl], in0=ft[:, fj, :], in1=zsT[:, hu, qsl], op=ALU.mult), [("ft", fj), zkey], [okey])
                pending.extend([s_o, s_sq, s_mmln, s_ex, s_g, s_z])
            else:
                T.op("act", lambda e: e.activation(out=lnb[:, fj, 0:CH], in_=psa(3, 0), func=AF.Ln), [psk(3, 0)], [("lnb", fj)])
                T.op("act", lambda e: e.activation(out=rsb[:, fj, 0:CH], in_=lnb[:, fj, 0:CH], func=AF.Exp, scale=-1.0), [("lnb", fj)], [("rsb", fj)])
                T.op("dve", lambda e: e.tensor_tensor(out=ft[:, fj, :], in0=psa(2, 0), in1=rsb[:, fj, 0:CH], op=ALU.mult), [psk(2, 0), ("rsb", fj)], [("ft", fj)])

                def s_z():
                    T.op("pool", lambda e: e.tensor_tensor(out=ogT[:, hu, qsl], in0=ft[:, fj, :], in1=zsT[:, hu, qsl], op=ALU.mult), [("ft", fj), zkey], [okey])
                pending.extend([s_z])

        s_emitted = 0
        for i in range(NIT):
            look = 1 if blocks[i // 32][1] == "A" else 2
            while s_emitted < min(NIT, i + 1 + look):
                nsrc = blocks[s_emitted // 32][3]
                if nsrc not in loaded:
                    load_kv(nsrc)
                if blocks[s_emitted // 32][1] == "A" and s_emitted > i + 1:
                    break
                emit_S(s_emitted)
                s_emitted += 1
            emit_PV(i)
            if i % 32 == 31:
                emit_finish(i)
            elif pending and i % 32 >= 2:
                pending.popleft()()
        while pending:
            pending.popleft()()
        T.barrier()
        if debug == 3:
            dtmp = view(arena, 0, [128, NOWN], F32)
            for hu in range(16):
                T.op("dve", lambda e, hu=hu: e.tensor_copy(out=dtmp, in_=ogT[:, hu, :]), [], ["dtmp"])
                T.dma("sp", dbg["og"][:, hu, :], dtmp, ["dtmp"], [], "d_dbg")
            T.barrier()
            return nc

        mT = view(arena, 0, [128, 16, NOWN], BF16)
        gsl = [[view(big, (s * 2 + g) * 8, [128, KC, 256], BF16) for g in range(2)] for s in range(2)]
        wo_ab = view(big, 32, [128, 2, 2, 8, 256], BF16)
        lnb = view(big, 48, [128, 2, 1024], F32)
        rsb = view(big, 56, [128, 2, 1024], F32)

        def load_p4(db):
            s_ = db % 2
            load_wblock(gsl[s_][0], w_in, C_GA + db * 256, 256, ("gsl", s_, 0), "d_gsl%d0" % s_)
            load_wblock(gsl[s_][1], w_in, C_GB + db * 256, 256, ("gsl", s_, 1), "d_gsl%d1" % s_)
            T.dma("pool", wo_ab[:, s_, 0], w_oa[:, db * 256:(db + 1) * 256].rearrange("(f p) n -> p f n", p=128), [], [("woab", s_, 0)], "d_woa%d" % s_)
            T.dma("pool", wo_ab[:, s_, 1], w_ob[:, db * 256:(db + 1) * 256].rearrange("(f p) n -> p f n", p=128), [], [("woab", s_, 1)], "d_wob%d" % s_)

        load_p4(0)
        for db in range(8):
            s = db % 2
            for c2 in range(2):
                dc = db * 2 + c2
                csl = slice(c2 * 128, (c2 + 1) * 128)
                for tc in range(2):
                    if c2 == 0 and tc == 1 and db + 1 < 8:
                        load_p4(db + 1)
                    tsl = slice(tc * CH, (tc + 1) * CH)
                    hk = [("hT_own", 0), ("hT_own", 1)]
                    pg, py = 2 * tc, 2 * tc + 1
                    proj_fm((pg, 0), lambda kc: gsl[s][0][:, kc, csl], lambda kc: hT_own[:, kc, tsl], hk + [("gsl", s, 0)])
                    proj_fm((pg, 1), lambda kc: gsl[s][1][:, kc, csl], lambda kc: hT_own[:, kc, tsl], hk + [("gsl", s, 1)])

                    def ya(e, br_, pb):
                        last = None
                        for fc in range(8):
                            last = e.matmul(psa(*pb), lhsT=wo_ab[:, s, br_, fc, csl], rhs=ogT[:, br_ * 8 + fc, tsl], start=(fc == 0), stop=(fc == 7))
                        return last
                    ogk = [("ogT", hh, tc) for hh in range(16)]
                    T.op("pe", lambda e: ya(e, 0, (py, 0)), ogk + [("woab", s, 0)], [psk(py, 0)])
                    T.op("pe", lambda e: ya(e, 1, (py, 1)), ogk + [("woab", s, 1)], [psk(py, 1)])
                    j = tc
                    T.op("act", lambda e: e.activation(out=lnb[:, j, :], in_=PS[pg][:], func=AF.Sigmoid), [psk(pg, 0), psk(pg, 1)], [("lnb", j)])
                    T.op("dve", lambda e: e.tensor_tensor(out=rsb[:, j, :], in0=PS[py][:], in1=lnb[:, j, :], op=ALU.mult), [psk(py, 0), psk(py, 1), ("lnb", j)], [("rsb", j)])
                    T.op("pool", lambda e: e.tensor_tensor(out=mT[:, dc, tsl], in0=rsb[:, j, 0:512], in1=rsb[:, j, 512:1024], op=ALU.add), [("rsb", j)], [("mT", dc, tc)])
        T.barrier()
        if debug == 4:
            dtmp = view(big, 0, [128, NOWN], F32)
            for hu in range(16):
                T.op("dve", lambda e, hu=hu: e.tensor_copy(out=dtmp, in_=mT[:, hu, :]), [], ["dtmp"])
                T.dma("sp", dbg["mg"][:, hu, :], dtmp, ["dtmp"], [], "d_dbg")
            T.barrier()
            return nc

        wo_sb = big[:, 0:32768].rearrange("p (k n) -> p k n", k=KC)
        junk5 = view(big, 64, [128, D], BF16)
        xs5 = view(arena, 32, [128, 2, D], F32)
        gbc = view(arena, 48, [128, D], F32)
        yb = bufB[:, 0:8192].bitcast(F32).rearrange("p (s n) -> p s n", s=2)
        for cb in range(4):
            T.dma("pool", wo_sb[:, :, cb * 512:(cb + 1) * 512], w_o[:, cb * 512:(cb + 1) * 512].rearrange("(k p) n -> p k n", p=128), [], [("wo", cb)], "d_wo%d" % cb)
        T.dma("sp", gbc, final_g.partition_broadcast(128), [], ["gbc"], "d_gbc")
        for tt in range(8):
            xb = tt % 2
            xk = ("xs", xb)
            T.dma("sp", xs5[:, xb, :], xkv[tt * 128:(tt + 1) * 128, :], [], [xk], "d_xs%d" % xb)
            yk = ("yb", xb)
            for cb in range(4):
                pb = (cb // 2, cb % 2) if tt % 2 == 0 else (2 + cb // 2, cb % 2)

                def f(e):
                    last = None
                    for kc in range(KC):
                        last = e.matmul(psa(*pb), lhsT=mT[:, kc, tt * 128:(tt + 1) * 128], rhs=wo_sb[:, kc, cb * 512:(cb + 1) * 512], start=(kc == 0), stop=(kc == KC - 1))
                    return last
                T.op("pe", f, [("mT", kc, tt // 4) for kc in range(KC)] + [("wo", cb)], [psk(*pb)])
                T.op("dve", lambda e: e.tensor_tensor(out=yb[:, xb, cb * 512:(cb + 1) * 512], in0=psa(*pb), in1=xs5[:, xb, cb * 512:(cb + 1) * 512], op=ALU.add),
                     [psk(*pb), xk], [yk])
            rstd, sk = rms_stats(yb[:, xb, :], yk, xb, junk5, "junk5")
            T.op("dve", lambda e: e.scalar_tensor_tensor(out=xs5[:, xb, :], in0=yb[:, xb, :], scalar=rstd, in1=gbc, op0=ALU.mult, op1=ALU.mult), [yk, sk, "gbc"], [xk])
            T.dma("sp", out[tt * 128:(tt + 1) * 128, :], xs5[:, xb, :], [xk], [], "d_out%d" % xb)
        T.barrier()
    return nc


def _rope_tables():
    inv = (1.0 / (10000.0 ** (np.arange(0, 64, 2, dtype=np.float32) / np.float32(64)))).astype(np.float32)
    pos = np.arange(S, dtype=np.float32)
    angA = pos[:, None] * inv[None, :]
    angR = (np.arange(S) // 64).astype(np.float32)[:, None] * inv[None, :]
    angC = (np.arange(S) % 64).astype(np.float32)[:, None] * inv[None, :]
    p = np.arange(128)
    f = p % 32
    sign = np.where((p % 64) < 32, -1.0, 1.0).astype(np.float32)
    cosA = np.cos(angA)[:, f].T.astype(np.float32)
    sinA = (np.sin(angA)[:, f].T * sign[:, None]).astype(np.float32)
    angB = np.where((p < 64)[None, :], angR[:, f], angC[:, f])
    cosB = np.cos(angB).T.astype(np.float32)
    sinB = (np.sin(angB).T * sign[:, None]).astype(np.float32)
    return cosA, sinA, cosB, sinB


def _const_mats():
    ident = np.eye(128, dtype=np.float32)
    p = np.arange(128)
    partner = np.where((p % 64) < 32, p + 32, p - 32)
    perm = np.zeros((128, 128), np.float32)
    perm[partner, p] = 1.0
    ones = np.ones((128, 128), np.float32)
    return np.ascontiguousarray(np.stack([ident, perm, ones], axis=1))


def make_in_maps(x, norm_g, w_in, lambda_q1, lambda_k1, lambda_q2, lambda_k2, subln_g, q_norm_g, k_norm_g,
                 w_out_a, w_out_b, w_o, final_g):
    f32 = lambda a: np.ascontiguousarray(np.asarray(a, dtype=np.float32))
    x = f32(x)
    tabs = _rope_tables()
    cm = _const_mats()
    lam4 = f32(np.concatenate([np.asarray(lambda_q1)[0], np.asarray(lambda_k1)[0], np.asarray(lambda_q2)[0], np.asarray(lambda_k2)[0]]))
    gvecs = f32(np.stack([np.asarray(subln_g)[0], np.asarray(q_norm_g)[0], np.asarray(k_norm_g)[0]], axis=1))
    w_in0, woa0, wob0, wo0 = f32(np.asarray(w_in)[0]), f32(np.asarray(w_out_a)[0]), f32(np.asarray(w_out_b)[0]), f32(np.asarray(w_o)[0])
    ng, fg = f32(np.asarray(norm_g)[0]), f32(final_g)
    maps = []
    for core in range(8):
        b, c = core // 4, core % 4
        order = np.roll(np.arange(S), -c * NOWN)
        m = {"xkv": np.ascontiguousarray(x[b][order]), "w_in": w_in0, "w_out_a": woa0, "w_out_b": wob0, "w_o": wo0,
             "norm_g": ng, "final_g": fg, "lam4": lam4, "gvecs": gvecs, "cmats": cm}
        for n, t in zip(("cosA", "sinA", "cosB", "sinB"), tabs):
            m[n] = np.ascontiguousarray(t[:, order])
        maps.append(m)
    return maps


def kernel(**inputs):
    nc = build_nc()
    maps = make_in_maps(**inputs)
    res = run_bass_kernel_spmd(nc, maps, core_ids=list(range(8)))
    out = np.empty((2, S, D), np.float32)
    for core in range(8):
        b, c = core // 4, core % 4
        out[b, c * NOWN:(c + 1) * NOWN] = res.results[core]["out"]
    return out
```
